# Optimizing a Trainium2 kernel written in Bass

```python
import math
import jax, jax.numpy as jnp
from jax import lax
import numpy as np

D_MODEL = 1024
BATCH = 8
SEQ = 2048
DEPTH = 4

CHUNK = 64
N_MIXERS = 2
N_SSD_LAYERS = (DEPTH + 1) // 2
N_ATTN_LAYERS = DEPTH // 2
EPS = 1e-6

SSD_EXPAND = 2
SSD_D_INNER = SSD_EXPAND * D_MODEL
SSD_HEAD_DIM = 64
SSD_HEADS = SSD_D_INNER // SSD_HEAD_DIM
SSD_GROUPS = 4
SSD_HEADS_PER_GROUP = SSD_HEADS // SSD_GROUPS
SSD_STATE = 128
SSD_CONV = 4
SSD_CONV_CH = SSD_D_INNER + 2 * SSD_GROUPS * SSD_STATE
SSD_IN = SSD_D_INNER + SSD_CONV_CH + SSD_HEADS

ATTN_HEADS = 8
ATTN_HEAD_DIM = D_MODEL // ATTN_HEADS // 2
ATTN_V_DIM = 2 * ATTN_HEAD_DIM
ATTN_IN = 3 * D_MODEL
ROPE_THETA = 500000.0
ROPE_DIM = ATTN_HEAD_DIM // 4
Q_BLOCK = 128

D_FF = 2816
FFN_CONV = 3

kernel_name = "hybrid_ssd_diffattn_convffn_trunk"


def rmsnorm(x, g):
    xf = x.astype(jnp.float32)
    y = xf * lax.rsqrt(jnp.mean(xf * xf, axis=-1, keepdims=True) + EPS)
    return (y * g).astype(x.dtype)


def causal_dwconv(x, w, b):
    k_width = w.shape[0]
    s = x.shape[1]
    xp = jnp.pad(x, ((0, 0), (k_width - 1, 0), (0, 0)))
    y = b
    for k in range(k_width):
        y = y + w[k] * xp[:, k:k + s]
    return y


def rope_tables(positions):
    inv_freq = ROPE_THETA ** (-jnp.arange(0, ROPE_DIM, 2, dtype=jnp.float32) / ROPE_DIM)
    ang = positions.astype(jnp.float32)[..., None] * inv_freq
    return jnp.cos(ang), jnp.sin(ang)


def apply_partial_rope(t, cos, sin):
    half = ROPE_DIM // 2
    c = cos[:, :, None, None, :]
    s = sin[:, :, None, None, :]
    tf = t.astype(jnp.float32)
    x1, x2, rest = tf[..., :half], tf[..., half:ROPE_DIM], tf[..., ROPE_DIM:]
    out = jnp.concatenate([x1 * c - x2 * s, x2 * c + x1 * s, rest], axis=-1)
    return out.astype(t.dtype)


def ssd_mixer(xn, in_w, conv_w, conv_b, dt_bias, a_log, d_skip, norm_g, out_w):
    b, s, _ = xn.shape
    nc = s // CHUNK
    G, R, P, N = SSD_GROUPS, SSD_HEADS_PER_GROUP, SSD_HEAD_DIM, SSD_STATE
    proj = xn @ in_w
    z = proj[..., :SSD_D_INNER]
    xbc = proj[..., SSD_D_INNER:SSD_D_INNER + SSD_CONV_CH]
    dt = proj[..., SSD_D_INNER + SSD_CONV_CH:]
    xbc = jax.nn.silu(causal_dwconv(xbc, conv_w, conv_b))
    xs = xbc[..., :SSD_D_INNER].reshape(b, nc, CHUNK, G, R, P)
    Bm = xbc[..., SSD_D_INNER:SSD_D_INNER + G * N].reshape(b, nc, CHUNK, G, N)
    Cm = xbc[..., SSD_D_INNER + G * N:].reshape(b, nc, CHUNK, G, N)
    dt = jax.nn.softplus(dt + dt_bias).reshape(b, nc, CHUNK, G, R)
    a = -jnp.exp(a_log).reshape(G, R)
    dA = (dt * a).transpose(0, 3, 4, 1, 2)
    x_dt = xs * dt[..., None]
    a_cs = jnp.cumsum(dA, axis=-1)
    seg = a_cs[..., :, None] - a_cs[..., None, :]
    tril = jnp.tril(jnp.ones((CHUNK, CHUNK), dtype=bool))
    lmat = jnp.where(tril, jnp.exp(jnp.where(tril, seg, -jnp.inf)), 0.0)
    cb = jnp.einsum("bclgn,bcsgn->bgcls", Cm, Bm)
    y_diag = jnp.einsum("bgrcls,bcsgrp->bclgrp", cb[:, :, None] * lmat, x_dt)
    decay = jnp.exp(a_cs[..., -1:] - a_cs)
    states = jnp.einsum("bclgn,bgrcl,bclgrp->cbgrpn", Bm, decay, x_dt)
    chunk_decay = jnp.exp(a_cs[..., -1]).transpose(3, 0, 1, 2)

    def step(h, inp):
        st, dec = inp
        return dec[..., None, None] * h + st, h

    _, h_in = lax.scan(step, jnp.zeros_like(states[0]), (states, chunk_decay))
    y_off = jnp.einsum("bclgn,cbgrpn,bgrcl->bclgrp", Cm, h_in, jnp.exp(a_cs))
    y = y_diag + y_off + xs * d_skip.reshape(G, R)[:, :, None]
    y = y.reshape(b, s, SSD_D_INNER)
    yg = (y * jax.nn.silu(z)).reshape(b, s, G, SSD_D_INNER // G)
    yg = rmsnorm(yg, 1.0).reshape(b, s, SSD_D_INNER) * norm_g
    return yg.astype(xn.dtype) @ out_w


def diff_attention(xn, cos, sin, in_w, q_norm_g, k_norm_g, lq1, lk1, lq2, lk2,
                   subln_g, out_w, lambda_init):
    b, s, _ = xn.shape
    H, HD = ATTN_HEADS, ATTN_HEAD_DIM
    proj = xn @ in_w
    q = proj[..., :D_MODEL].reshape(b, s, H, 2, HD)
    k = proj[..., D_MODEL:2 * D_MODEL].reshape(b, s, H, 2, HD)
    v = proj[..., 2 * D_MODEL:].reshape(b, s, H, ATTN_V_DIM)
    q = apply_partial_rope(rmsnorm(q, q_norm_g), cos, sin)
    k = apply_partial_rope(rmsnorm(k, k_norm_g), cos, sin)
    lam = (jnp.exp(jnp.sum(lq1.astype(jnp.float32) * lk1.astype(jnp.float32)))
           - jnp.exp(jnp.sum(lq2.astype(jnp.float32) * lk2.astype(jnp.float32)))
           + lambda_init)
    scale = 1.0 / math.sqrt(HD)
    outs = []
    for blk in range(s // Q_BLOCK):
        q0, q1 = blk * Q_BLOCK, (blk + 1) * Q_BLOCK
        qb = q[:, q0:q1]
        kc = k[:, :q1]
        vc = v[:, :q1]
        sc = jnp.einsum("bqhcd,bkhcd->bhcqk", qb, kc).astype(jnp.float32) * scale
        q_chunk = (q0 + jnp.arange(Q_BLOCK)) // CHUNK
        k_chunk = jnp.arange(q1) // CHUNK
        allowed = k_chunk[None, :] <= q_chunk[:, None]
        sc = jnp.where(allowed, sc, -jnp.inf)
        p = jax.nn.softmax(sc, axis=-1)
        attn = p[:, :, 0] - lam * p[:, :, 1]
        outs.append(jnp.einsum("bhqk,bkhe->bqhe", attn.astype(v.dtype), vc))
    o = jnp.concatenate(outs, axis=1)
    o = rmsnorm(o, subln_g) * (1.0 - lambda_init)
    return o.reshape(b, s, H * ATTN_V_DIM).astype(xn.dtype) @ out_w


def conv_ffn(xn, up_w, conv_w, conv_b, down_w):
    h = causal_dwconv(xn @ up_w, conv_w, conv_b)
    g, u = h[..., :D_FF], h[..., D_FF:]
    return (jax.nn.silu(g) * u) @ down_w


def setup_inputs(seed: int = 0) -> dict:
    key = jax.random.key(seed)
    ks = jax.random.split(key, 32)
    f32 = jnp.float32

    def nrm(k, shape, std):
        return jax.random.normal(k, shape, f32) * std

    def gain(k, shape):
        return 1.0 + 0.05 * jax.random.normal(k, shape, f32)

    out_scale = 1.0 / math.sqrt(2 * DEPTH)
    x = jax.random.normal(ks[0], (BATCH, SEQ, D_MODEL), f32)
    start = jax.random.randint(ks[1], (BATCH,), 0, 64) * CHUNK
    positions = (start[:, None] + jnp.arange(SEQ)[None, :]).astype(jnp.int32)
    dt0 = jnp.exp(jax.random.uniform(ks[6], (N_SSD_LAYERS, SSD_HEADS), f32,
                                     math.log(0.001), math.log(0.1)))
    return {
        "x": x,
        "positions": positions,
        "norm_mix_g": gain(ks[2], (DEPTH, D_MODEL)),
        "norm_ffn_g": gain(ks[3], (DEPTH, D_MODEL)),
        "ssd_in_w": nrm(ks[4], (N_SSD_LAYERS, D_MODEL, SSD_IN), D_MODEL ** -0.5),
        "ssd_conv_w": nrm(ks[5], (N_SSD_LAYERS, SSD_CONV, SSD_CONV_CH), SSD_CONV ** -0.5),
        "ssd_conv_b": nrm(ks[7], (N_SSD_LAYERS, SSD_CONV_CH), 0.02),
        "ssd_dt_bias": dt0 + jnp.log(-jnp.expm1(-dt0)),
        "ssd_a_log": jnp.log(jax.random.uniform(ks[8], (N_SSD_LAYERS, SSD_HEADS), f32, 1.0, 16.0)),
        "ssd_d": gain(ks[9], (N_SSD_LAYERS, SSD_HEADS)),
        "ssd_norm_g": gain(ks[10], (N_SSD_LAYERS, SSD_D_INNER)),
        "ssd_out_w": nrm(ks[11], (N_SSD_LAYERS, SSD_D_INNER, D_MODEL), SSD_D_INNER ** -0.5 * out_scale),
        "attn_in_w": nrm(ks[12], (N_ATTN_LAYERS, D_MODEL, ATTN_IN), D_MODEL ** -0.5),
        "attn_q_norm_g": gain(ks[13], (N_ATTN_LAYERS, ATTN_HEAD_DIM)),
        "attn_k_norm_g": gain(ks[14], (N_ATTN_LAYERS, ATTN_HEAD_DIM)),
        "attn_lq1": nrm(ks[15], (N_ATTN_LAYERS, ATTN_HEAD_DIM), 0.1),
        "attn_lk1": nrm(ks[16], (N_ATTN_LAYERS, ATTN_HEAD_DIM), 0.1),
        "attn_lq2": nrm(ks[17], (N_ATTN_LAYERS, ATTN_HEAD_DIM), 0.1),
        "attn_lk2": nrm(ks[18], (N_ATTN_LAYERS, ATTN_HEAD_DIM), 0.1),
        "attn_subln_g": gain(ks[19], (N_ATTN_LAYERS, ATTN_V_DIM)),
        "attn_out_w": nrm(ks[20], (N_ATTN_LAYERS, D_MODEL, D_MODEL), D_MODEL ** -0.5 * out_scale),
        "ffn_up_w": nrm(ks[21], (DEPTH, D_MODEL, 2 * D_FF), D_MODEL ** -0.5),
        "ffn_conv_w": nrm(ks[22], (DEPTH, FFN_CONV, 2 * D_FF), FFN_CONV ** -0.5),
        "ffn_conv_b": nrm(ks[23], (DEPTH, 2 * D_FF), 0.02),
        "ffn_down_w": nrm(ks[24], (DEPTH, D_FF, D_MODEL), D_FF ** -0.5 * out_scale),
    }


def reference(x, positions, norm_mix_g, norm_ffn_g,
              ssd_in_w, ssd_conv_w, ssd_conv_b, ssd_dt_bias, ssd_a_log, ssd_d,
              ssd_norm_g, ssd_out_w,
              attn_in_w, attn_q_norm_g, attn_k_norm_g, attn_lq1, attn_lk1,
              attn_lq2, attn_lk2, attn_subln_g, attn_out_w,
              ffn_up_w, ffn_conv_w, ffn_conv_b, ffn_down_w):
    cos, sin = rope_tables(positions)
    for i in range(DEPTH):
        j = i // N_MIXERS
        h = rmsnorm(x, norm_mix_g[i])
        if i % N_MIXERS == 0:
            x = x + ssd_mixer(h, ssd_in_w[j], ssd_conv_w[j], ssd_conv_b[j],
                              ssd_dt_bias[j], ssd_a_log[j], ssd_d[j],
                              ssd_norm_g[j], ssd_out_w[j])
        else:
            lambda_init = 0.8 - 0.6 * math.exp(-0.3 * i)
            x = x + diff_attention(h, cos, sin, attn_in_w[j], attn_q_norm_g[j],
                                   attn_k_norm_g[j], attn_lq1[j], attn_lk1[j],
                                   attn_lq2[j], attn_lk2[j], attn_subln_g[j],
                                   attn_out_w[j], lambda_init)
        h = rmsnorm(x, norm_ffn_g[i])
        x = x + conv_ffn(h, ffn_up_w[i], ffn_conv_w[i], ffn_conv_b[i], ffn_down_w[i])
    return x
```

```python
import math
from contextlib import ExitStack

import numpy as np
import concourse.bass as bass
import concourse.mybir as mybir
from concourse.bass_utils import run_bass_kernel_spmd

F32 = mybir.dt.float32
BF16 = mybir.dt.bfloat16
I32 = mybir.dt.int32
AF = mybir.ActivationFunctionType
ALU = mybir.AluOpType

D = 1024
S = 2048
KC = D // 128
DEPTH = 4
EPS = 1e-6
D_FF = 2816
NFC = D_FF // 128
SSD_DI = 2048
SSD_CONVCH = 3072
SSD_IN = 5152
SSD_H = 32
ATT_H = 8

SAME_ENGINE_SYNC = True
INV_FREQ = (500000.0 ** (-np.arange(0, 16, 2, dtype=np.float32) / np.float32(16))).astype(np.float32)


class _Op:
    __slots__ = ("eng", "fn", "reads", "writes", "chan", "deps", "needs_inc", "inc_idx", "waits", "idx")

    def __init__(self, eng, fn, reads, writes, chan):
        self.eng = eng
        self.fn = fn
        self.reads = reads
        self.writes = writes
        self.chan = chan
        self.deps = set()
        self.needs_inc = False
        self.inc_idx = 0
        self.waits = []


class Prog:
    ENGS = ("pe", "act", "dve", "pool", "sp")

    def __init__(self, nc, stack):
        self.nc = nc
        self.stack = stack
        self.ops = []
        self.final_chans = []

    def sb(self, name, shape, dtype):
        return self.stack.enter_context(self.nc.sbuf_tensor(name, list(shape), dtype))

    def ps(self, name, shape, dtype=F32):
        return self.stack.enter_context(self.nc.psum_tensor(name, list(shape), dtype))

    def op(self, eng, fn, reads=(), writes=(), chan=None):
        o = _Op(eng, fn, tuple(reads), tuple(writes), chan)
        o.idx = len(self.ops)
        self.ops.append(o)
        return o

    def barrier(self):
        self.ops.append("barrier")

    def emit(self):
        nc = self.nc
        ops = self.ops
        last_w = {}
        readers = {}
        chan_last = {}
        eng_last = {}
        pending_bar = None
        bar_done = set()
        real = []
        for o in ops:
            if isinstance(o, str):
                pending_bar = list(eng_last.values()) + list(chan_last.values())
                bar_done = set()
                continue
            real.append(o)
            deps = o.deps
            if pending_bar is not None and o.eng not in bar_done:
                deps.update(pending_bar)
                bar_done.add(o.eng)
            for r in o.reads:
                w = last_w.get(r)
                if w is not None:
                    deps.add(w)
            for w_ in o.writes:
                w = last_w.get(w_)
                if w is not None:
                    deps.add(w)
                for rd in readers.get(w_, {}).values():
                    deps.add(rd)
            if o.chan is not None:
                p = chan_last.get(o.chan)
                if p is not None:
                    deps.add(p)
                chan_last[o.chan] = o
            else:
                eng_last[o.eng] = o
            rk = o.eng if o.chan is None else ("c", o.idx)
            for r in o.reads:
                readers.setdefault(r, {})[rk] = o
            for w_ in o.writes:
                last_w[w_] = o
                readers[w_] = {}
            deps.discard(o)
        for o in real:
            for d in o.deps:
                if d.chan is None:
                    if d.eng != o.eng or (SAME_ENGINE_SYNC and d.eng != "pe"):
                        d.needs_inc = True
        eng_cnt = {e: 0 for e in self.ENGS}
        chan_cnt = {}
        last_of_eng = {}
        for o in real:
            if o.chan is not None:
                chan_cnt[o.chan] = chan_cnt.get(o.chan, 0) + 1
                o.inc_idx = 16 * chan_cnt[o.chan]
            else:
                last_of_eng[o.eng] = o
        for e, o in last_of_eng.items():
            o.needs_inc = True
        for o in real:
            if o.chan is None and o.needs_inc:
                eng_cnt[o.eng] += 1
                o.inc_idx = eng_cnt[o.eng]
        sems = {}
        for e in self.ENGS:
            sems[("eng", e)] = self.stack.enter_context(nc.semaphore("sem_" + e))
        for c in chan_cnt:
            sems[("chan", c)] = self.stack.enter_context(nc.semaphore("semc_" + str(c)))
        self.n_sems = len(sems)
        waited = {e: {} for e in self.ENGS}
        for o in real:
            wl = {}
            for d in o.deps:
                if d.chan is not None:
                    key = ("chan", d.chan)
                else:
                    if d.eng == o.eng and (d.eng == "pe" or not SAME_ENGINE_SYNC):
                        continue
                    key = ("eng", d.eng)
                v = d.inc_idx
                if v > wl.get(key, 0):
                    wl[key] = v
            for key, v in wl.items():
                if waited[o.eng].get(key, 0) >= v:
                    continue
                waited[o.eng][key] = v
                o.waits.append((sems[key], v))
        final_waits = []
        for e in self.ENGS:
            if eng_cnt[e] > 0:
                final_waits.append((sems[("eng", e)], eng_cnt[e]))
        for c, n in chan_cnt.items():
            final_waits.append((sems[("chan", c)], 16 * n))
        per_eng = {e: [o for o in real if o.eng == e] for e in self.ENGS}
        self.stats = {e: len(per_eng[e]) for e in self.ENGS}
        self.stats["waits"] = sum(len(o.waits) for o in real)
        self.stats["incs"] = dict(eng_cnt)

        block = self.stack.enter_context(nc.Block())

        def run(eng_obj, lst, fin=None):
            for o in lst:
                for (sm, v) in o.waits:
                    eng_obj.wait_ge(sm, v)
                inst = o.fn(eng_obj)
                if o.chan is not None:
                    inst.then_inc(sems[("chan", o.chan)], 16)
                elif o.needs_inc:
                    inst.then_inc(sems[("eng", o.eng)], 1)
            if fin:
                for (sm, v) in fin:
                    eng_obj.wait_ge(sm, v)

        @block.tensor
        def _(e):
            run(e, per_eng["pe"])

        @block.scalar
        def _(e):
            run(e, per_eng["act"])

        @block.vector
        def _(e):
            run(e, per_eng["dve"])

        @block.gpsimd
        def _(e):
            run(e, per_eng["pool"])

        @block.sync
        def _(e):
            run(e, per_eng["sp"], final_waits)


def pipeline(stages, n, skew=1, order=None):
    ns = len(stages)
    order = order or list(range(ns))
    for t in range(n + (ns - 1) * skew):
        for s in order:
            i = t - s * skew
            if 0 <= i < n:
                stages[s](i)


class Ring:
    def __init__(self, items):
        self.items = items
        self.i = -1

    def next(self):
        self.i = (self.i + 1) % len(self.items)
        return self.items[self.i]


class Model:
    def __init__(self, nc, stack, layers, dbg=None):
        self.nc = nc
        self.stack = stack
        self.P = Prog(nc, stack)
        self.layers = layers
        self.dbg = dbg or {}
        self.dram = {}
        self.uid = 0
        P = self.P
        self.xT = P.sb("xT_sb", [128, KC, S], F32)
        self.consts_f = P.sb("cf_sb", [128, 2048], F32)
        self.ident_f = P.sb("identf_sb", [128, 128], F32)
        self.ident_b = P.sb("identb", [128, 128], BF16)
        self.ones_b = P.sb("onesb", [128, 128], BF16)
        self.AW = 33 * 1024
        self.arena = P.sb("arena", [128, self.AW], F32)
        self.apos = 0
        self.psum = P.ps("psum", [128, 8, 512], F32)
        self.cf_pos = 0

    def din(self, name, shape, dtype=F32):
        t = self.nc.dram_tensor(name, list(shape), dtype, kind="ExternalInput")
        self.dram[name] = t
        return t.ap()

    def dout(self, name, shape, dtype=F32):
        t = self.nc.dram_tensor(name, list(shape), dtype, kind="ExternalOutput")
        self.dram[name] = t
        return t.ap()

    def reset_arena(self):
        self.apos = 0

    def alloc(self, shape, dtype):
        n = 1
        for s_ in shape:
            n *= s_
        nbytes = n * (4 if dtype in (F32, I32) else 2)
        words = (nbytes + 3) // 4
        words = (words + 7) // 8 * 8
        a = self.apos
        self.apos += words
        assert self.apos <= self.AW, f"arena overflow {self.apos} > {self.AW}"
        v = self.arena[:, a:a + words]
        if dtype != F32:
            v = v.bitcast(dtype)
        v = v[:, 0:n]
        if len(shape) == 2:
            v = v.rearrange("p (a b) -> p a b", a=shape[0])
        elif len(shape) == 3:
            v = v.rearrange("p (a b c) -> p a b c", a=shape[0], b=shape[1])
        return v

    def key(self, base):
        self.uid += 1
        return f"{base}#{self.uid}"

    def cf_alloc(self, n):
        a = self.cf_pos
        self.cf_pos += n
        assert self.cf_pos <= 2048
        return self.consts_f[:, a:a + n]

    def dma(self, eng, out, in_, reads, writes, chan, **kw):
        self.P.op(eng, lambda e, out=out, in_=in_, kw=kw: e.dma_start(out=out, in_=in_, **kw),
                  reads=reads, writes=writes, chan=chan)

    def mm(self, out, lhsT, rhs, start, stop, reads, writes):
        self.P.op("pe", lambda e, out=out, lhsT=lhsT, rhs=rhs, start=start, stop=stop:
                  e.matmul(out, lhsT, rhs, start=start, stop=stop), reads=reads, writes=writes)

    def transpose(self, out, in_, ident, reads, writes):
        self.P.op("pe", lambda e, out=out, in_=in_, ident=ident: e.transpose(out, in_, ident),
                  reads=reads, writes=writes)

    def act(self, out, in_, func, reads, writes, scale=1.0, bias=0.0, eng="act"):
        self.P.op(eng, lambda e, out=out, in_=in_, func=func, scale=scale, bias=bias:
                  e.activation(out=out, in_=in_, func=func, scale=scale, bias=bias),
                  reads=reads, writes=writes)

    def ts(self, eng, out, in0, s1, op0, reads, writes, s2=None, op1=None):
        if op1 is None:
            self.P.op(eng, lambda e, out=out, in0=in0, s1=s1, op0=op0:
                      e.tensor_scalar(out=out, in0=in0, scalar1=s1, scalar2=None, op0=op0),
                      reads=reads, writes=writes)
        else:
            self.P.op(eng, lambda e, out=out, in0=in0, s1=s1, op0=op0, s2=s2, op1=op1:
                      e.tensor_scalar(out=out, in0=in0, scalar1=s1, scalar2=s2, op0=op0, op1=op1),
                      reads=reads, writes=writes)

    def stt(self, eng, out, in0, scalar, in1, op0, op1, reads, writes):
        self.P.op(eng, lambda e, out=out, in0=in0, scalar=scalar, in1=in1, op0=op0, op1=op1:
                  e.scalar_tensor_tensor(out=out, in0=in0, scalar=scalar, in1=in1, op0=op0, op1=op1),
                  reads=reads, writes=writes)

    def tt(self, eng, out, in0, in1, op, reads, writes):
        self.P.op(eng, lambda e, out=out, in0=in0, in1=in1, op=op:
                  e.tensor_tensor(out=out, in0=in0, in1=in1, op=op), reads=reads, writes=writes)

    def copy(self, eng, out, in_, reads, writes):
        if eng == "act":
            self.P.op(eng, lambda e, out=out, in_=in_: e.copy(out=out, in_=in_), reads=reads, writes=writes)
        else:
            self.P.op(eng, lambda e, out=out, in_=in_: e.tensor_copy(out=out, in_=in_), reads=reads, writes=writes)

    def memset(self, eng, ap, val, writes):
        self.P.op(eng, lambda e, ap=ap, val=val: e.memset(ap, val), reads=(), writes=writes)

    def recip(self, out, in_, reads, writes):
        self.P.op("dve", lambda e, out=out, in_=in_: e.reciprocal(out=out, in_=in_), reads=reads, writes=writes)

    def declare(self):
        self.d_xT = self.din("xT", [D, S])
        self.d_out = self.dout("outT", [D, S])
        self.d_cf = self.din("cf", [128, 2048])
        self.d_identf = self.din("identf", [128, 128])
        self.d_ffn_up = self.din("ffn_up_w", [DEPTH, D, 2 * D_FF])
        self.d_ffn_dn = self.din("ffn_down_w", [DEPTH, D_FF, D])
        self.d_pos = self.din("pos", [128, 16], I32)
        self.d_arow = self.din("arow", [2, 128, 520])
        self.d_attn_in = self.din("attn_in_w", [2, D, 3 * D])
        self.d_attn_out = self.din("attn_out_w", [2, D, D])
        self.d_srow = self.din("srow", [2, 128, 96])
        self.d_masks = self.din("masks", [3, 128, 128])
        self.d_ssd_in = self.din("ssd_in_w", [2, D, SSD_IN])
        self.d_ssd_out = self.din("ssd_out_w", [2, SSD_DI, D])

    def prologue(self):
        for kc in range(KC):
            self.dma("sp", self.xT[:, kc, :], self.d_xT[kc * 128:(kc + 1) * 128, :],
                     reads=(), writes=[("xT", kc, tb) for tb in range(4)], chan=f"x{kc % 4}")
        self.dma("sp", self.consts_f[:], self.d_cf[:, :], reads=(), writes=["cf"], chan="c0")
        self.dma("sp", self.ident_f[:], self.d_identf[:, :], reads=(), writes=["identf"], chan="c1")
        self.copy("dve", self.ident_b[:], self.ident_f[:], reads=["identf"], writes=["identb"])
        self.U_f = self.P.sb("U_f", [128, 128], F32)
        self.SL_f = self.P.sb("SL_f", [128, 128], F32)
        self.ones_f = self.P.sb("ones_f", [128, 128], F32)
        for i, t_ in enumerate((self.U_f, self.SL_f, self.ones_f)):
            self.dma("sp", t_[:], self.d_masks[i], reads=(), writes=["masks"], chan=f"x{i}")
        self.memset("dve", self.ones_b[:], 1.0, writes=["onesb"])

    def epilogue(self):
        for kc in range(KC):
            self.dma("sp", self.d_out[kc * 128:(kc + 1) * 128, :], self.xT[:, kc, :],
                     reads=[("xT", kc, tb) for tb in range(4)], writes=(), chan=f"x{kc % 4}")

    CF_NMIX = 0
    CF_NFFN = 32
    CF_FCW = 64
    CF_FCB = 592
    CF_END = 768

    def norm(self, gbase, hn, sq, rstd, hn_key="hn"):
        for tb in range(4):
            tsl = slice(tb * 512, (tb + 1) * 512)
            self.act(sq[:, :, :], self.xT[:, :, tsl], AF.Square,
                     reads=[("xT", kc, tb) for kc in range(KC)], writes=["sq"])
            pn = self.psum[:, 7, :]
            for kc in range(KC):
                self.mm(pn, self.ones_b[:], sq[:, kc, :], kc == 0, kc == KC - 1,
                        reads=["sq", "onesb"], writes=["ps7"])
            self.act(rstd[:, :], pn, AF.Sqrt, reads=["ps7", "cf"], writes=["rstd"], scale=1.0 / D,
                     bias=self.eps_col)
            self.recip(rstd[:, :], rstd[:, :], reads=["rstd"], writes=["rstd"])
            for kc in range(KC):
                self.stt("dve", hn[:, kc, tsl], self.xT[:, kc, tsl], self.consts_f[:, gbase + kc:gbase + kc + 1],
                         rstd[:, :], ALU.mult, ALU.mult,
                         reads=[("xT", kc, tb), "rstd", "cf"], writes=[(hn_key, kc, tb)])

    def ffn(self, l):
        self.P.barrier()
        self.reset_arena()
        hn = self.alloc([KC, S], BF16)
        sq = self.alloc([KC, 512], BF16)
        rstd = self.alloc([512], F32)
        J = 4
        wg = [self.alloc([KC, J * 128], BF16) for _ in range(2)]
        wu = [self.alloc([KC, J * 128], BF16) for _ in range(2)]
        wd = [self.alloc([J, D], BF16) for _ in range(1)]
        hb = self.alloc([J, S], BF16)
        gr = [self.alloc([1026], F32) for _ in range(2)]
        ur = [self.alloc([1026], F32) for _ in range(2)]
        tg = self.alloc([1024], F32)
        tu = self.alloc([1024], F32)
        dtmp = [self.alloc([512], F32) for _ in range(2)]
        self.norm(self.CF_NFFN + l * 8, hn, sq, rstd)
        groups = [(f0, min(J, NFC - f0)) for f0 in range(0, NFC, J)]
        up = self.d_ffn_up[l].rearrange("(kc p) n -> p kc n", p=128)
        dn = self.d_ffn_dn[l]
        st = {"psi": 0, "pdi": 0}
        units = []
        for gi, (f0, nj) in enumerate(groups):
            for j in range(nj):
                for half in range(2):
                    units.append((gi, j, half))
        first_of = {}
        last_of = {}
        for ui, (gi, j, half) in enumerate(units):
            first_of.setdefault(gi, ui)
            last_of[gi] = ui

        def load_up(gi):
            f0, nj = groups[gi]
            b = gi % 2
            self.dma("pool", wg[b][:, :, 0:nj * 128], up[:, :, f0 * 128:(f0 + nj) * 128],
                     reads=(), writes=[("wg", b)], chan=f"wg{b}")
            self.dma("pool", wu[b][:, :, 0:nj * 128], up[:, :, D_FF + f0 * 128:D_FF + (f0 + nj) * 128],
                     reads=(), writes=[("wu", b)], chan=f"wu{b}")

        def load_dn(gi):
            f0, nj = groups[gi]
            self.dma("pool", wd[0][:, 0:nj, :], dn[f0 * 128:(f0 + nj) * 128, :].rearrange("(j p) n -> p j n", p=128),
                     reads=(), writes=[("wd", 0)], chan="wd0")

        def part_a(ui):
            gi, j, half = units[ui]
            f0, nj = groups[gi]
            b = gi % 2
            ub = ui % 2
            for (raw, wt, wkey, rkey) in ((gr, wg, "wg", "gr"), (ur, wu, "wu", "ur")):
                for t2 in range(2):
                    tb = half * 2 + t2
                    pb = st["psi"] % 4
                    st["psi"] += 1
                    pt = self.psum[:, pb, :]
                    for kc in range(KC):
                        self.mm(pt, wt[b][:, kc, j * 128:(j + 1) * 128], hn[:, kc, tb * 512:(tb + 1) * 512],
                                kc == 0, kc == KC - 1,
                                reads=[(wkey, b), ("hn", kc, tb)], writes=[("psu", pb)])
                    self.copy("act", raw[ub][:, 2 + t2 * 512:2 + (t2 + 1) * 512], pt,
                              reads=[("psu", pb)], writes=[(rkey, ub, 1 + t2)])
                if half == 0:
                    self.memset("dve", raw[ub][:, 0:2], 0.0, writes=[(rkey, ub, 0)])
                else:
                    self.copy("dve", raw[ub][:, 0:2], raw[1 - ub][:, 1024:1026],
                              reads=[(rkey, 1 - ub, 2)], writes=[(rkey, ub, 0)])

        def part_b(ui):
            gi, j, half = units[ui]
            f0, nj = groups[gi]
            fc = f0 + j
            ub = ui % 2
            for (raw, tt_, cc, rkey, tkey) in ((gr, tg, fc, "gr", "tg"), (ur, tu, NFC + fc, "ur", "tu")):
                wcol = lambda k, cc=cc: self.consts_f[:, self.CF_FCW + l * 132 + k * 44 + cc:
                                                      self.CF_FCW + l * 132 + k * 44 + cc + 1]
                bcol = self.consts_f[:, self.CF_FCB + l * 44 + cc:self.CF_FCB + l * 44 + cc + 1]
                rr = [(rkey, ub, 0), (rkey, ub, 1), (rkey, ub, 2), "cf"]
                self.ts("dve", tt_[:, :], raw[ub][:, 2:1026], wcol(2), ALU.mult, reads=rr, writes=[tkey],
                        s2=bcol, op1=ALU.add)
                self.stt("dve", tt_[:, :], raw[ub][:, 1:1025], wcol(1), tt_[:, :], ALU.mult, ALU.add,
                         reads=rr + [tkey], writes=[tkey])
                self.stt("dve", tt_[:, :], raw[ub][:, 0:1024], wcol(0), tt_[:, :], ALU.mult, ALU.add,
                         reads=rr + [tkey], writes=[tkey])
            self.act(tg[:, :], tg[:, :], AF.Silu, reads=["tg"], writes=["tg"])
            self.tt("dve", hb[:, j, half * 1024:(half + 1) * 1024], tg[:, :], tu[:, :], ALU.mult,
                    reads=["tg", "tu"], writes=[("hb", j, half)])

        def down_fn(gi):
            f0, nj = groups[gi]
            for dc in range(KC):
                for tb in range(4):
                    pb = 4 + st["pdi"] % 2
                    st["pdi"] += 1
                    pt = self.psum[:, pb, :]
                    for j in range(nj):
                        self.mm(pt, wd[0][:, j, dc * 128:(dc + 1) * 128], hb[:, j, tb * 512:(tb + 1) * 512],
                                j == 0, j == nj - 1, reads=[("wd", 0), ("hb", j, tb // 2)], writes=[("psd", pb)])
                    xs = self.xT[:, dc, tb * 512:(tb + 1) * 512]
                    di = st["pdi"] % 2
                    self.copy("act", dtmp[di][:, :], pt, reads=[("psd", pb)], writes=[("dtmp", di)])
                    self.tt("pool", xs, dtmp[di][:, :], xs, ALU.add, reads=[("dtmp", di), ("xT", dc, tb)], writes=[("xT", dc, tb)])

        load_up(0)
        a_done = set()
        for gi in range(len(groups)):
            load_dn(gi)
            if gi + 1 < len(groups):
                load_up(gi + 1)
            for ui in range(first_of[gi], last_of[gi] + 1):
                if ui not in a_done:
                    part_a(ui)
                    a_done.add(ui)
                part_b(ui)
            if gi + 1 < len(groups):
                nu = first_of[gi + 1]
                part_a(nu)
                a_done.add(nu)
            down_fn(gi)

    def rope_tables(self):
        self.cos_t = self.P.sb("cos_t", [128, 16, 8], F32)
        self.sin_t = self.P.sb("sin_t", [128, 16, 8], F32)
        self.reset_arena()
        posi = self.alloc([16], I32)
        posf = self.alloc([16], F32)
        invf = self.alloc([8], F32)
        ang = self.alloc([16, 8], F32)
        kf = self.alloc([16, 8], F32)
        ki = self.alloc([16, 8], I32)
        y = self.alloc([16, 8], F32)
        t = self.alloc([16, 8], F32)
        self.dma("sp", posi[:, :], self.d_pos[:, :], reads=(), writes=["posi"], chan="c0")
        self.dma("sp", invf[:, :], self.d_arow[0, :, 512:520], reads=(), writes=["invf"], chan="c1")
        self.copy("dve", posf[:, :], posi[:, :], reads=["posi"], writes=["posf"])
        for i in range(16):
            self.ts("dve", ang[:, i, :], invf[:, :], posf[:, i:i + 1], ALU.mult, reads=["posf", "invf"], writes=["ang"])
        TWO_PI = 2.0 * math.pi
        C1 = 6.28125
        C2 = TWO_PI - C1
        self.ts("dve", kf[:, :, :], ang[:, :, :], 1.0 / TWO_PI, ALU.mult, reads=["ang"], writes=["kf"])
        self.copy("dve", ki[:, :, :], kf[:, :, :], reads=["kf"], writes=["ki"])
        self.copy("dve", kf[:, :, :], ki[:, :, :], reads=["ki"], writes=["kf"])
        self.stt("dve", y[:, :, :], kf[:, :, :], -C1, ang[:, :, :], ALU.mult, ALU.add, reads=["kf", "ang"], writes=["y"])
        self.stt("dve", y[:, :, :], kf[:, :, :], -C2, y[:, :, :], ALU.mult, ALU.add, reads=["kf", "y"], writes=["y"])
        for (dst, shift) in ((self.sin_t, 0.0), (self.cos_t, math.pi / 2)):
            z = t
            self.ts("dve", z[:, :, :], y[:, :, :], shift, ALU.add, reads=["y"], writes=["z"])
            self.ts("dve", kf[:, :, :], z[:, :, :], math.pi, ALU.is_gt, reads=["z"], writes=["kf"], s2=-TWO_PI, op1=ALU.mult)
            self.tt("dve", z[:, :, :], z[:, :, :], kf[:, :, :], ALU.add, reads=["z", "kf"], writes=["z"])
            self.ts("dve", kf[:, :, :], z[:, :, :], -math.pi, ALU.is_lt, reads=["z"], writes=["kf"], s2=TWO_PI, op1=ALU.mult)
            self.tt("dve", z[:, :, :], z[:, :, :], kf[:, :, :], ALU.add, reads=["z", "kf"], writes=["z"])
            self.ts("dve", z[:, :, :], z[:, :, :], math.pi, ALU.min, reads=["z"], writes=["z"], s2=-math.pi, op1=ALU.max)
            self.act(dst[:, :, :], z[:, :, :], AF.Sin, reads=["z"], writes=["rope_tab"])

    def attn(self, l):
        j = l // 2
        lambda_init = 0.8 - 0.6 * math.exp(-0.3 * l)
        self.P.barrier()
        self.reset_arena()
        A = self.alloc
        kT = A([8, S], BF16)
        v1 = A([16, 8, 130], BF16)
        hn = A([KC, 512], BF16)
        qT = A([8, 512], BF16)
        oT = A([8, 512], BF16)
        wr = [A([KC, 512], BF16) for _ in range(2)]
        sq = oT
        rstd = A([512], F32)
        arow = A([520], F32)
        xf = [A([512], F32) for _ in range(3)]
        sqt = rstd
        knb = [A([512], BF16) for _ in range(2)]
        ss8s = [A([8], F32) for _ in range(2)]
        ss8 = ss8s[0]
        rt = [A([8, 8], F32) for _ in range(4)]
        Eb = [A([2, 256], BF16) for _ in range(3)]
        oall = [A([8, 128], F32) for _ in range(2)]
        onb = A([8, 128], BF16)
        rc4s = [A([4], F32) for _ in range(2)]
        lam = A([4], F32)
        gq = A([64], F32)
        gsub = A([128], F32)
        lt = A([64], F32)
        self.dma("sp", arow[:, :], self.d_arow[j], reads=(), writes=["arow"], chan="c0")
        self.memset("dve", v1[:, :, :, 128:129], 1.0, writes=["v1ones"])
        self.ts("dve", gq[:, :], arow[:, 0:64], 0.125, ALU.mult, reads=["arow"], writes=["gq"])
        self.ts("dve", gsub[:, :], arow[:, 384:512], 1.0 - lambda_init, ALU.mult, reads=["arow"], writes=["gsub"])
        for i, (a0, b0) in enumerate(((128, 192), (256, 320))):
            self.P.op("dve", lambda e, i=i, a0=a0, b0=b0: e.tensor_tensor(out=lt[:, :], in0=arow[:, a0:a0 + 64], in1=arow[:, b0:b0 + 64], op=ALU.mult),
                      reads=["arow"], writes=["lt"])
            self.P.op("dve", lambda e, i=i: e.reduce_sum(out=lam[:, i:i + 1], in_=lt[:, :], axis=mybir.AxisListType.X),
                      reads=["lt"], writes=["lam"])
        self.act(lam[:, 0:2], lam[:, 0:2], AF.Exp, reads=["lam"], writes=["lam"])
        self.tt("dve", lam[:, 2:3], lam[:, 1:2], lam[:, 0:1], ALU.subtract, reads=["lam"], writes=["lam"])
        self.ts("dve", lam[:, 3:4], lam[:, 2:3], -lambda_init, ALU.add, reads=["lam"], writes=["lam"])
        neg_lam = lam[:, 3:4]

        win = self.d_attn_in[j].rearrange("(kc p) n -> p kc n", p=128)
        wout = self.d_attn_out[j].rearrange("(kc p) n -> p kc n", p=128)
        psb = self.psum.bitcast(BF16)
        wi = 0
        pr = 0

        def nextbank():
            nonlocal pr
            b = pr % 4
            pr += 1
            return b

        for tb in range(4):
            t0 = tb * 512
            self.act(sq[:, :, :], self.xT[:, :, t0:t0 + 512], AF.Square,
                     reads=[("xT", kc, tb) for kc in range(KC)], writes=["oT"])
            b = nextbank()
            pn = self.psum[:, b, :]
            for kc in range(KC):
                self.mm(pn, self.ones_b[:], sq[:, kc, :], kc == 0, kc == KC - 1, reads=["oT", "onesb"], writes=[("ps", b)])
            self.act(rstd[:, :], pn, AF.Sqrt, reads=[("ps", b), "cf"], writes=["rstd"], scale=1.0 / D, bias=self.eps_col)
            self.recip(rstd[:, :], rstd[:, :], reads=["rstd"], writes=["rstd"])
            for kc in range(KC):
                g0 = self.CF_NMIX + l * 8 + kc
                self.stt("dve", hn[:, kc, :], self.xT[:, kc, t0:t0 + 512], self.consts_f[:, g0:g0 + 1], rstd[:, :],
                         ALU.mult, ALU.mult, reads=[("xT", kc, tb), "rstd", "cf"], writes=["hn"])
            units = [(cb, tt_) for cb in (4, 5, 0, 1, 2, 3) for tt_ in range(4)]
            ustate = {}

            def s1(i, units=units, tb=tb, ustate=ustate):
                nonlocal wi
                cb, tt_ = units[i]
                if tt_ == 0:
                    ustate[cb] = wi % 2
                    self.dma("pool", wr[wi % 2][:, :, :], win[:, :, cb * 512:(cb + 1) * 512], reads=(), writes=[("wr", wi % 2)],
                             chan=f"wr{wi % 2}")
                    wi += 1
                w = wr[ustate[cb]]
                wkey = ("wr", ustate[cb])
                kind = cb // 2
                hb = (cb % 2) * 4
                kt = tb * 4 + tt_
                b = nextbank()
                ustate[("b", i)] = b
                pt = self.psum[:, b, :]
                for kc in range(KC):
                    self.mm(pt, hn[:, kc, tt_ * 128:(tt_ + 1) * 128], w[:, kc, :], kc == 0, kc == KC - 1,
                            reads=["hn", wkey], writes=[("ps", b)])
                if kind == 2:
                    self.copy("act", v1[:, kt, hb:hb + 4, 0:128], pt.rearrange("p (h e) -> p h e", h=4),
                              reads=[("ps", b)], writes=[("v1", kt)])
                else:
                    self.copy("act", xf[i % 3][:, :], pt, reads=[("ps", b)], writes=[("xf", i % 3)])

            def s2a(i, units=units, tb=tb):
                cb, tt_ = units[i]
                if cb // 2 == 2:
                    return
                x_ = xf[i % 3]
                xk = ("xf", i % 3)
                s8 = ss8s[i % 2]
                sk = ("ss8", i % 2)
                self.tt("dve", sqt[:, :], x_[:, :], x_[:, :], ALU.mult, reads=[xk], writes=["rstd"])
                self.P.op("dve", lambda e, s8=s8: e.reduce_sum(out=s8[:, :], in_=sqt[:, :].rearrange("p (g d) -> p g d", g=8),
                                                               axis=mybir.AxisListType.X), reads=["rstd"], writes=[sk])
                self.act(s8[:, :], s8[:, :], AF.Sqrt, reads=[sk, "cf"], writes=[sk], scale=1.0 / 64, bias=self.eps_col)

            def s2b(i, units=units, tb=tb):
                cb, tt_ = units[i]
                kind = cb // 2
                if kind == 2:
                    return
                kt = tb * 4 + tt_
                x_ = xf[i % 3]
                xk = ("xf", i % 3)
                s8 = ss8s[i % 2]
                sk = ("ss8", i % 2)
                self.recip(s8[:, :], s8[:, :], reads=[sk], writes=[sk])
                x3 = x_[:, :].rearrange("p (g d) -> p g d", g=8)
                self.tt("dve", x3, x3, s8[:, :].unsqueeze(2).to_broadcast([128, 8, 64]), ALU.mult,
                        reads=[xk, sk], writes=[xk])
                gsrc = gq[:, :] if kind == 0 else arow[:, 64:128]
                self.tt("dve", x3, x3, gsrc.unsqueeze(1).to_broadcast([128, 8, 64]), ALU.mult,
                        reads=[xk, "gq", "arow"], writes=[xk])
                Aap = x3[:, :, 0:8]
                Bap = x3[:, :, 8:16]
                Cc = self.cos_t[:, kt, :].unsqueeze(1).to_broadcast([128, 8, 8])
                Sn = self.sin_t[:, kt, :].unsqueeze(1).to_broadcast([128, 8, 8])
                kk = ("knb", i % 2)
                k3 = knb[i % 2][:, :].rearrange("p (g d) -> p g d", g=8)
                self.copy("dve", knb[i % 2][:, :], x_[:, :], reads=[xk], writes=[kk])
                self.tt("pool", rt[0][:, :, :], Aap, Cc, ALU.mult, reads=[xk, "rope_tab"], writes=["rt0"])
                self.tt("pool", rt[1][:, :, :], Bap, Sn, ALU.mult, reads=[xk, "rope_tab"], writes=["rt1"])
                self.tt("pool", rt[2][:, :, :], Bap, Cc, ALU.mult, reads=[xk, "rope_tab"], writes=["rt2"])
                self.tt("pool", rt[3][:, :, :], Aap, Sn, ALU.mult, reads=[xk, "rope_tab"], writes=["rt3"])
                self.tt("pool", k3[:, :, 0:8], rt[0][:, :, :], rt[1][:, :, :], ALU.subtract, reads=["rt0", "rt1", kk], writes=[kk])
                self.tt("pool", k3[:, :, 8:16], rt[2][:, :, :], rt[3][:, :, :], ALU.add, reads=["rt2", "rt3", kk], writes=[kk])

            def s3(i, units=units, tb=tb):
                cb, tt_ = units[i]
                kind = cb // 2
                if kind == 2:
                    return
                hb = (cb % 2) * 4
                kt = tb * 4 + tt_
                kb_ = knb[i % 2]
                kk = ("knb", i % 2)
                b2 = nextbank()
                for hh in range(4):
                    self.transpose(psb[:, b2, hh * 128:(hh + 1) * 128], kb_[:, hh * 128:(hh + 1) * 128], self.ident_b[:],
                                   reads=[kk, "identb"], writes=[("ps", b2)])
                src_ = psb[:, b2, 0:512].rearrange("p (h t) -> p h t", h=4)
                if kind == 0:
                    self.copy("act", qT[:, hb:hb + 4, tt_ * 128:(tt_ + 1) * 128], src_, reads=[("ps", b2)], writes=[("qT", tt_)])
                else:
                    self.copy("act", kT[:, hb:hb + 4, kt * 128:(kt + 1) * 128], src_, reads=[("ps", b2)], writes=[("kT", kt)])

            pipeline([s1, s2a, s2b, s3], len(units))
            for sub in range(2):
                qt0 = tb * 4 + sub * 2
                nk = qt0 + 2
                items = [(h, kt) for h in range(ATT_H) for kt in range(nk)]

                def stage_a(i, items=items, qt0=qt0, sub=sub):
                    h, kt = items[i]
                    if kt <= qt0:
                        qlo, n = 0, 256
                    else:
                        qlo, n = 128, 128
                    sb_ = i % 2
                    E = Eb[i % 3]
                    ek = ("E", i % 3)
                    for c in range(2):
                        self.mm(self.psum[:, 2 * sb_ + c, 0:n], kT[c * 64:(c + 1) * 64, h, kt * 128:(kt + 1) * 128],
                                qT[c * 64:(c + 1) * 64, h, sub * 256 + qlo:sub * 256 + qlo + n], True, True,
                                reads=[("kT", kt), ("qT", sub * 2), ("qT", sub * 2 + 1)], writes=[("ps", 2 * sb_ + c)])
                    self.act(E[:, :, qlo:qlo + n], self.psum[:, 2 * sb_:2 * sb_ + 2, 0:n], AF.Exp,
                             reads=[("ps", 2 * sb_), ("ps", 2 * sb_ + 1)], writes=[ek])
                    if kt >= qt0:
                        c0 = (kt - qt0) * 128
                        self.memset("dve", E[64:128, :, c0:c0 + 64], 0.0, writes=[ek])

                def stage_b(i, items=items, qt0=qt0, sub=sub, nk=nk):
                    h, kt = items[i]
                    E = Eb[i % 3]
                    ek = ("E", i % 3)
                    for qi in range(2):
                        qt = qt0 + qi
                        if kt > qt:
                            continue
                        for c in range(2):
                            ob = 4 + qi * 2 + c
                            self.mm(self.psum[:, ob, 0:129], E[:, c, qi * 128:(qi + 1) * 128], v1[:, kt, h, 0:129],
                                    kt == 0, kt == qt, reads=[ek, ("v1", kt), "v1ones"], writes=[("ps", ob)])
                    if kt != nk - 1:
                        return
                    rc4 = rc4s[h % 2]
                    rk4 = ("rc4", h % 2)
                    self.recip(rc4[:, :], self.psum[:, 4:8, 128], reads=[("ps", 4), ("ps", 5), ("ps", 6), ("ps", 7)], writes=[rk4])
                    r2 = rc4[:, :].rearrange("p (q c) -> p q c", c=2)
                    self.ts("dve", r2[:, :, 1], r2[:, :, 1], neg_lam, ALU.mult, reads=[rk4, "lam"], writes=[rk4])
                    for qi in range(2):
                        ob = 4 + qi * 2
                        oa = oall[qi]
                        self.act(oa[:, h, :], self.psum[:, ob, 0:128], AF.Copy, reads=[("ps", ob), rk4],
                                 writes=[("oall", qi, h)], scale=rc4[:, 2 * qi:2 * qi + 1])
                        self.stt("dve", oa[:, h, :], self.psum[:, ob + 1, 0:128], rc4[:, 2 * qi + 1:2 * qi + 2], oa[:, h, :],
                                 ALU.mult, ALU.add,
                                 reads=[("ps", ob + 1), rk4, ("oall", qi, h)], writes=[("oall", qi, h)])

                pipeline([stage_a, stage_b], len(items), skew=2)
                for qi in range(2):
                    oa = oall[qi]
                    ok = ("oall", qi)
                    okh = [("oall", qi, hh_) for hh_ in range(8)]
                    o2 = oa[:, :, :]
                    sq4 = sqt[:, :].rearrange("p (g d) -> p g d", g=4)
                    for hf in range(2):
                        self.tt("dve", sq4, oa[:, hf * 4:hf * 4 + 4, :], oa[:, hf * 4:hf * 4 + 4, :], ALU.mult,
                                reads=okh + [("ss8", 0)], writes=["rstd"])
                        self.P.op("dve", lambda e, hf=hf, sq4=sq4: e.reduce_sum(out=ss8[:, hf * 4:hf * 4 + 4], in_=sq4,
                                                                                axis=mybir.AxisListType.X),
                                  reads=["rstd"], writes=[("ss8", 0)])
                    self.act(ss8[:, :], ss8[:, :], AF.Sqrt, reads=[("ss8", 0), "cf"], writes=[("ss8", 0)], scale=1.0 / 128, bias=self.eps_col)
                    self.recip(ss8[:, :], ss8[:, :], reads=[("ss8", 0)], writes=[("ss8", 0)])
                    self.tt("dve", o2, o2, ss8[:, :].unsqueeze(2).to_broadcast([128, 8, 128]), ALU.mult, reads=okh + [("ss8", 0)], writes=okh)
                    self.tt("dve", onb[:, :, :], o2, gsub[:, :].unsqueeze(1).to_broadcast([128, 8, 128]), ALU.mult,
                            reads=okh + ["gsub"], writes=["onb"])
                    b2 = nextbank()
                    for hh in range(8):
                        self.transpose(psb[:, b2, hh * 128:(hh + 1) * 128], onb[:, hh, :], self.ident_b[:],
                                       reads=["onb", "identb"], writes=[("ps", b2)])
                    c0 = sub * 256 + qi * 128
                    self.copy("act", oT[:, :, c0:c0 + 128], psb[:, b2, :].rearrange("p (h t) -> p h t", h=8),
                              reads=[("ps", b2)], writes=["oT"])
            for half in range(2):
                w = wr[wi % 2]
                wkey = ("wr", wi % 2)
                wi += 1
                self.dma("pool", w[:, :, :], wout[:, :, half * 512:(half + 1) * 512], reads=(), writes=[wkey], chan=f"wr{(wi - 1) % 2}")
                for dcc in range(4):
                    dc = half * 4 + dcc
                    b = nextbank()
                    pt = self.psum[:, b, :]
                    for kc in range(KC):
                        self.mm(pt, w[:, kc, dcc * 128:(dcc + 1) * 128], oT[:, kc, :], kc == 0, kc == KC - 1,
                                reads=[wkey, "oT"], writes=[("ps", b)])
                    xs = self.xT[:, dc, t0:t0 + 512]
                    self.tt("dve", xs, pt, xs, ALU.add, reads=[("ps", b), ("xT", dc, tb)], writes=[("xT", dc, tb)])

    CF_SCW = 768
    CF_SCB = 960
    CF_SNG = 1008

    def ssd(self, l):
        j = l // 2
        P = self.P
        P.barrier()
        self.reset_arena()
        A = self.alloc
        sz = A([4, 2048], BF16)
        xdt = A([4, 2048], BF16)
        BT = A([4, 512], BF16)
        CT = A([4, 512], BF16)
        Btok = A([4, 4, 128], BF16)
        dt = A([4, 32], F32)
        dA = A([4, 32], F32)
        halo = A([24, 4], F32)
        hT = A([2048], F32)
        hTb = A([2048], BF16)
        srow = A([96], F32)
        arow_ = A([32], F32)
        wdt = A([KC, 32], BF16)
        wr = [A([KC, 512], BF16) for _ in range(2)]
        ygT = A([16, 512], BF16)
        mark = self.apos
        self.dma("sp", srow[:, :], self.d_srow[j], reads=(), writes=["srow"], chan="c0")
        self.act(arow_[:, :], srow[:, 32:64], AF.Exp, reads=["srow"], writes=["arow_"])
        self.ts("dve", arow_[:, :], arow_[:, :], -1.0, ALU.mult, reads=["arow_"], writes=["arow_"])
        self.memset("dve", halo[:, :, :], 0.0, writes=[("halo", ch_) for ch_ in range(24)])
        self.memset("dve", hT[:, :], 0.0, writes=[("hT", g_) for g_ in range(4)])
        self.memset("dve", hTb[:, :], 0.0, writes=[("hTb", g_) for g_ in range(4)])
        win = self.d_ssd_in[j].rearrange("(kc p) n -> p kc n", p=128)
        wout = self.d_ssd_out[j].rearrange("(kc p) n -> p kc n", p=128)
        self.dma("pool", wdt[:, :, :], win[:, :, 5120:5152], reads=(), writes=["wdt"], chan="wdt")
        psb = self.psum.bitcast(BF16)
        pr = 0
        wblocks = []
        for tb_ in range(4):
            for cb in range(4):
                wblocks.append(("in", win[:, :, cb * 512:(cb + 1) * 512]))
            for cb in range(6):
                wblocks.append(("in", win[:, :, 2048 + cb * 512:2048 + (cb + 1) * 512]))
            for q4 in range(4):
                wblocks.append(("out", wout[:, :, q4 * 256:(q4 + 1) * 256]))
        wstate = {"next": 0, "use": 0}

        def wfetch():
            k = wstate["next"]
            if k >= len(wblocks):
                return
            kind, src_ = wblocks[k]
            dst = wr[k % 2][:, :, :] if kind == "in" else wr[k % 2].rearrange("p a (b c) -> p (a b) c", b=2)
            self.dma("pool", dst, src_, reads=(), writes=[("wr", k % 2)], chan=f"wr{k % 2}")
            wstate["next"] = k + 1

        def wuse():
            k = wstate["use"]
            wstate["use"] = k + 1
            while wstate["next"] <= min(k + 1, len(wblocks) - 1):
                wfetch()
            return wr[k % 2], ("wr", k % 2)

        def nextbank(nb=4, base=0):
            nonlocal pr
            b = base + pr % nb
            pr += 1
            return b

        for tb in range(4):
            t0 = tb * 512
            P.barrier()
            self.apos = mark
            hn = A([KC, 512], BF16)
            sq = A([KC, 512], BF16)
            rstd = A([512], F32)
            raw = [A([516], F32) for _ in range(2)]
            tcv = [A([512], F32) for _ in range(3)]
            xch = [A([512], BF16) for _ in range(2)]
            dtt = A([4, 32], F32)
            self.act(sq[:, :, :], self.xT[:, :, t0:t0 + 512], AF.Square,
                     reads=[("xT", kc, tb) for kc in range(KC)], writes=["sq"])
            b = nextbank()
            pn = self.psum[:, b, :]
            for kc in range(KC):
                self.mm(pn, self.ones_b[:], sq[:, kc, :], kc == 0, kc == KC - 1, reads=["sq", "onesb"], writes=[("ps", b)])
            self.act(rstd[:, :], pn, AF.Sqrt, reads=[("ps", b), "cf"], writes=["rstd"], scale=1.0 / D, bias=self.eps_col)
            self.recip(rstd[:, :], rstd[:, :], reads=["rstd"], writes=["rstd"])
            for kc in range(KC):
                g0 = self.CF_NMIX + l * 8 + kc
                self.stt("dve", hn[:, kc, :], self.xT[:, kc, t0:t0 + 512], self.consts_f[:, g0:g0 + 1], rstd[:, :],
                         ALU.mult, ALU.mult, reads=[("xT", kc, tb), "rstd", "cf"], writes=["hn"])
            for tt_ in range(4):
                b = nextbank()
                pt = self.psum[:, b, 0:32]
                for kc in range(KC):
                    self.mm(pt, hn[:, kc, tt_ * 128:(tt_ + 1) * 128], wdt[:, kc, :], kc == 0, kc == KC - 1,
                            reads=["hn", "wdt"], writes=[("ps", b)])
                self.tt("dve", dtt[:, tt_, :], pt, srow[:, 0:32], ALU.add, reads=[("ps", b), "srow"], writes=["dtt"])
            d3 = dt[:, :, :]
            x3 = dtt[:, :, :]
            self.stt("dve", d3, x3, -1.0, x3, ALU.mult, ALU.max, reads=["dtt"], writes=["dt"])
            self.act(d3, d3, AF.Exp, reads=["dt"], writes=["dt"], scale=-1.0)
            self.act(d3, d3, AF.Ln, reads=["dt"], writes=["dt"], bias=self.one_col)
            self.ts("dve", x3, x3, 0.0, ALU.max, reads=["dtt"], writes=["dtt"])
            self.tt("dve", d3, d3, x3, ALU.add, reads=["dt", "dtt"], writes=["dt"])
            self.tt("dve", dA[:, :, :], d3, arow_[:, :].unsqueeze(1).to_broadcast([128, 4, 32]), ALU.mult,
                    reads=["dt", "arow_"], writes=["dA"])
            for cb in range(4):
                w, wkey = wuse()
                for tt_ in range(4):
                    b = nextbank()
                    pt = self.psum[:, b, :]
                    for kc in range(KC):
                        self.mm(pt, hn[:, kc, tt_ * 128:(tt_ + 1) * 128], w[:, kc, :], kc == 0, kc == KC - 1,
                                reads=["hn", wkey], writes=[("ps", b)])
                    self.act(sz[:, tt_, cb * 512:(cb + 1) * 512], pt, AF.Silu, reads=[("ps", b)], writes=[("sz", tt_)])
            wslot = {}

            def p1(ch):
                cb, cl = divmod(ch, 4)
                if cl == 0:
                    wslot[cb] = wuse()
                w, wkey = wslot[cb]
                r_ = raw[ch % 2]
                rk = ("raw", ch % 2)
                b = nextbank()
                pt = self.psum[:, b, :]
                for kc in range(KC):
                    self.mm(pt, w[:, kc, cl * 128:(cl + 1) * 128], hn[:, kc, :], kc == 0, kc == KC - 1,
                            reads=["hn", wkey], writes=[("ps", b)])
                self.copy("pool", r_[:, 0:3], halo[:, ch, 0:3], reads=[("halo", ch)], writes=[rk])
                self.copy("act", r_[:, 3:515], pt, reads=[("ps", b)], writes=[rk])
                wc3 = self.consts_f[:, self.CF_SCW + j * 96 + 3 * 24 + ch:self.CF_SCW + j * 96 + 3 * 24 + ch + 1]
                bc3 = self.consts_f[:, self.CF_SCB + j * 24 + ch:self.CF_SCB + j * 24 + ch + 1]
                self.act(tcv[ch % 3][:, :], pt, AF.Identity, reads=[("ps", b), "cf"], writes=[("tcv", ch % 3)], scale=wc3, bias=bc3)
                self.copy("pool", halo[:, ch, 0:3], r_[:, 512:515], reads=[rk], writes=[("halo", ch)])

            def p2a(ch):
                r_ = raw[ch % 2]
                rk = ("raw", ch % 2)
                tc_ = tcv[ch % 3]
                tk = ("tcv", ch % 3)
                wc = lambda k, ch=ch: self.consts_f[:, self.CF_SCW + j * 96 + k * 24 + ch:self.CF_SCW + j * 96 + k * 24 + ch + 1]
                for k in (2, 1, 0):
                    self.stt("dve", tc_[:, :], r_[:, k:k + 512], wc(k), tc_[:, :], ALU.mult, ALU.add,
                             reads=[rk, "cf", tk], writes=[tk])

            def p2b(ch):
                tc_ = tcv[ch % 3]
                tk = ("tcv", ch % 3)
                if ch < 16:
                    self.act(xch[ch % 2][:, :], tc_[:, :], AF.Silu, reads=[tk], writes=[("xch", ch % 2)])
                elif ch < 20:
                    self.act(BT[:, ch - 16, :], tc_[:, :], AF.Silu, reads=[tk], writes=[("BT", ch - 16)])
                else:
                    self.act(CT[:, ch - 20, :], tc_[:, :], AF.Silu, reads=[tk], writes=[("CT", ch - 20)])

            def p3(ch):
                if ch < 16:
                    xc_ = xch[ch % 2]
                    xk = ("xch", ch % 2)
                    b2 = nextbank()
                    for tt_ in range(4):
                        self.transpose(psb[:, b2, tt_ * 128:(tt_ + 1) * 128], xc_[:, tt_ * 128:(tt_ + 1) * 128], self.ident_b[:],
                                       reads=[xk, "identb"], writes=[("ps", b2)])
                    self.tt("dve", xdt[:, :, ch * 128:(ch + 1) * 128].rearrange("p t (h d) -> p t h d", h=2),
                            psb[:, b2, 0:512].rearrange("p (t h d) -> p t h d", t=4, h=2),
                            dt[:, :, 2 * ch:2 * ch + 2].unsqueeze(3).to_broadcast([128, 4, 2, 64]), ALU.mult,
                            reads=[("ps", b2), "dt"], writes=[("xdt", ch)])
                elif ch < 20:
                    g = ch - 16
                    b2 = nextbank()
                    for tt_ in range(4):
                        self.transpose(psb[:, b2, tt_ * 128:(tt_ + 1) * 128], BT[:, g, tt_ * 128:(tt_ + 1) * 128], self.ident_b[:],
                                       reads=[("BT", g), "identb"], writes=[("ps", b2)])
                    self.copy("act", Btok[:, :, g, :], psb[:, b2, 0:512].rearrange("p (t n) -> p t n", t=4),
                              reads=[("ps", b2)], writes=[("Btok", g)])

            pipeline([p1, p2a, p2b, p3], 24)
            P.barrier()
            self.apos = mark
            Rg = [A([8, 128], F32) for _ in range(2)]
            Lm = [A([8, 128], BF16) for _ in range(2)]
            MT = [A([8, 128], BF16) for _ in range(2)]
            CBU = A([4, 128], BF16)
            xdd = A([2048], BF16)
            ytmp = [A([512], F32) for _ in range(2)]
            ynb = [A([512], BF16) for _ in range(2)]
            sm = A([64], F32)
            ea = A([32], F32)
            dec = A([32], F32)
            etot = A([32], F32)
            ddt = A([32], F32)
            ss1 = A([4], F32)
            tmpD = [A([8, 128], BF16) for _ in range(3)]
            ddt4 = A([4, 32], F32)
            pre = [dict(sm=A([64], F32), ea=A([32], F32), dec=A([32], F32), etot=A([32], F32), ddt=A([32], F32),
                        CBU=A([4, 128], BF16)) for _ in range(2)]

            def preamble(c):
                pp = pre[c % 2]
                pk = lambda n: (n, c % 2)
                self.mm(self.psum[:, 0, 0:32], self.U_f[:], dA[:, c, :], True, True, reads=["dA", "masks"], writes=[("ps", 0)])
                self.mm(self.psum[:, 0, 32:64], self.ones_f[:], dA[:, c, :], True, True, reads=["dA", "masks"], writes=[("ps", 0)])
                self.copy("dve", pp["sm"][:, :], self.psum[:, 0, 0:64], reads=[("ps", 0)], writes=[pk("sm")])
                self.act(pp["ea"][:, :], pp["sm"][:, 0:32], AF.Exp, reads=[pk("sm")], writes=[pk("ea")])
                self.tt("dve", pp["dec"][:, :], pp["sm"][:, 32:64], pp["sm"][:, 0:32], ALU.subtract, reads=[pk("sm")], writes=[pk("dec")])
                self.act(pp["dec"][:, :], pp["dec"][:, :], AF.Exp, reads=[pk("dec")], writes=[pk("dec")])
                self.act(pp["etot"][:, :], pp["sm"][:, 32:64], AF.Exp, reads=[pk("sm")], writes=[pk("etot")])
                for g in range(4):
                    self.mm(self.psum[:, 1, g * 128:(g + 1) * 128], BT[:, g, c * 128:(c + 1) * 128], CT[:, g, c * 128:(c + 1) * 128],
                            True, True, reads=[("BT", g), ("CT", g)], writes=[("ps", 1)])
                self.tt("dve", pp["CBU"][:, :, :], self.psum[:, 1, :].rearrange("p (g l) -> p g l", g=4),
                        self.U_f[:].unsqueeze(1).to_broadcast([128, 4, 128]), ALU.mult, reads=[("ps", 1), "masks"], writes=[pk("CBU")])

            def state_update(c):
                pp = pre[c % 2]
                pk = lambda n: (n, c % 2)
                for g in range(4):
                    sbk = 7 if g % 2 == 0 else 0
                    self.mm(self.psum[:, sbk, :], Btok[:, c, g, :], xdd[:, g * 512:(g + 1) * 512], True, True,
                            reads=[("Btok", g), "xdd"], writes=[("ps", sbk)])
                    hg = hT[:, g * 512:(g + 1) * 512]
                    self.tt("dve", hg.rearrange("p (r d) -> p r d", r=8), hg.rearrange("p (r d) -> p r d", r=8),
                            pp["etot"][:, g * 8:(g + 1) * 8].unsqueeze(2).to_broadcast([128, 8, 64]), ALU.mult,
                            reads=[("hT", g), pk("etot")], writes=[("hT", g)])
                    self.tt("dve", hg, self.psum[:, sbk, :], hg, ALU.add, reads=[("ps", sbk), ("hT", g)], writes=[("hT", g)])
                    self.copy("act", hTb[:, g * 512:(g + 1) * 512], hg, reads=[("hT", g)], writes=[("hTb", g)])

            self.recip(ddt4[:, :, :], dt[:, :, :], reads=["dt"], writes=["ddt4"])
            self.tt("dve", ddt4[:, :, :], ddt4[:, :, :], srow[:, 64:96].unsqueeze(1).to_broadcast([128, 4, 32]), ALU.mult,
                    reads=["ddt4", "srow"], writes=["ddt4"])

            def s0(i):
                c, g = divmod(i, 4)
                kt = tb * 4 + c
                if g == 0:
                    preamble(c)
                bi = i % 2
                R = Rg[bi]
                self.tt("pool", R[:, :, :], dA[:, c, g * 8:(g + 1) * 8].unsqueeze(2).to_broadcast([128, 8, 128]),
                        self.U_f[:].unsqueeze(1).to_broadcast([128, 8, 128]), ALU.mult, reads=["dA", "masks"], writes=[("R", bi)])
                TD = tmpD[i % 3]
                self.tt("pool", TD[:, :, :], self.ident_b[:].unsqueeze(1).to_broadcast([128, 8, 128]),
                        ddt4[:, c, g * 8:(g + 1) * 8].unsqueeze(2).to_broadcast([128, 8, 128]), ALU.mult,
                        reads=["identb", "ddt4"], writes=[("tmpD", i % 3)])
                if g == 3 and kt != 15:
                    pp = pre[c % 2]
                    self.tt("pool", xdd[:, :].rearrange("p (h d) -> p h d", h=32), xdt[:, c, :].rearrange("p (h d) -> p h d", h=32),
                            pp["dec"][:, :].unsqueeze(2).to_broadcast([128, 32, 64]), ALU.mult,
                            reads=[("xdt", q) for q in range(16)] + [("dec", c % 2)], writes=["xdd"])

            def s1(i):
                bi = i % 2
                R = Rg[bi]
                for k2 in range(2):
                    self.mm(self.psum[:, 2 + k2, :], self.SL_f[:], R[:, 4 * k2:4 * k2 + 4, :].rearrange("p r l -> p (r l)"),
                            True, True, reads=[("R", bi), "masks"], writes=[("ps", 2 + k2)])
                self.act(Lm[bi][:, :, :], self.psum[:, 2:4, :].rearrange("p b (r l) -> p (b r) l", r=4), AF.Exp,
                         reads=[("ps", 2), ("ps", 3)], writes=[("Lm", bi)])

            def s2(i):
                c, g = divmod(i, 4)
                bi = i % 2
                pp = pre[c % 2]
                M_ = MT[bi]
                self.tt("dve", M_[:, :, :], Lm[bi][:, :, :], pp["CBU"][:, g, :].unsqueeze(1).to_broadcast([128, 8, 128]), ALU.mult,
                        reads=[("Lm", bi), ("CBU", c % 2)], writes=[("MT", bi)])
                self.tt("pool", M_[:, :, :], M_[:, :, :], tmpD[i % 3][:, :, :], ALU.add, reads=[("MT", bi), ("tmpD", i % 3)],
                        writes=[("MT", bi)])

            def s3(i):
                c, g = divmod(i, 4)
                kt = tb * 4 + c
                bi = i % 2
                M_ = MT[bi]
                for r in range(8):
                    h = g * 8 + r
                    self.mm(self.psum[:, 4, r * 64:(r + 1) * 64], M_[:, r, :], xdt[:, c, h * 64:(h + 1) * 64], True, True,
                            reads=[("MT", bi), ("xdt", h // 2)], writes=[("ps", 4)])
                self.mm(self.psum[:, 5, :], CT[:, g, c * 128:(c + 1) * 128], hTb[:, g * 512:(g + 1) * 512], True, True,
                        reads=[("CT", g), ("hTb", g)], writes=[("ps", 5)])
                if g == 3 and kt != 15:
                    state_update(c)

            def s4(i):
                c, g = divmod(i, 4)
                bi = i % 2
                pp = pre[c % 2]
                yt = ytmp[bi]
                yk = ("ytmp", bi)
                self.tt("dve", yt[:, :].rearrange("p (r d) -> p r d", r=8), self.psum[:, 5, :].rearrange("p (r d) -> p r d", r=8),
                        pp["ea"][:, g * 8:(g + 1) * 8].unsqueeze(2).to_broadcast([128, 8, 64]), ALU.mult,
                        reads=[("ps", 5), ("ea", c % 2)], writes=[yk])
                self.tt("dve", yt[:, :], self.psum[:, 4, :], yt[:, :], ALU.add, reads=[("ps", 4), yk], writes=[yk])
                self.tt("dve", yt[:, :], yt[:, :], sz[:, c, g * 512:(g + 1) * 512], ALU.mult, reads=[yk, ("sz", c)], writes=[yk])
                sk = ("ss1", i % 4)
                s_ = ss1[:, i % 4:i % 4 + 1]
                self.P.op("act", lambda e, yt=yt, s_=s_, bi=bi: e.activation(out=ynb[bi][:, :], in_=yt[:, :], func=AF.Square, accum_out=s_),
                          reads=[yk], writes=[("ynb", bi), sk])
                self.act(s_, s_, AF.Sqrt, reads=[sk, "cf"], writes=[sk], scale=1.0 / 512, bias=self.eps_col)
                self.recip(s_, s_, reads=[sk], writes=[sk])
                self.act(ynb[bi][:, :], yt[:, :], AF.Copy, reads=[yk, sk], writes=[("ynb", bi)], scale=s_)

            def s5(i):
                c, g = divmod(i, 4)
                bi = i % 2
                yb = ynb[bi]
                for q in range(4):
                    self.transpose(psb[:, 6, q * 128:(q + 1) * 128], yb[:, q * 128:(q + 1) * 128], self.ident_b[:],
                                   reads=[("ynb", bi), "identb"], writes=[("ps", 6)])
                for q in range(4):
                    cc = g * 4 + q
                    g0 = self.CF_SNG + j * 16 + cc
                    self.act(ygT[:, cc, c * 128:(c + 1) * 128], psb[:, 6, q * 128:(q + 1) * 128], AF.Copy,
                             reads=[("ps", 6), "cf"], writes=["ygT"], scale=self.consts_f[:, g0:g0 + 1])

            pipeline([s0, s1, s2, s3, s4, s5], 16, order=[5, 4, 3, 2, 1, 0])
            for q4 in range(4):
                w, wkey = wuse()
                w16 = w.rearrange("p a (b c) -> p (a b) c", b=2)
                for dcc in range(2):
                    dc = q4 * 2 + dcc
                    b = nextbank(2, 2)
                    pt = self.psum[:, b, :]
                    for kc in range(16):
                        self.mm(pt, w16[:, kc, dcc * 128:(dcc + 1) * 128], ygT[:, kc, :], kc == 0, kc == 15,
                                reads=[wkey, "ygT"], writes=[("ps", b)])
                    xs = self.xT[:, dc, t0:t0 + 512]
                    self.tt("dve", xs, pt, xs, ALU.add, reads=[("ps", b), ("xT", dc, tb)], writes=[("xT", dc, tb)])

def build(layers, stages=("ffn",)):
    nc = bass.Bass("TRN2", target_bir_lowering=False)
    stack = ExitStack()
    m = Model(nc, stack, layers)
    m.declare()
    m.eps_col = None
    m.prologue()
    m.eps_col = m.consts_f[:, 2047:2048]
    m.one_col = m.consts_f[:, 2046:2047]
    m.rope_tables()
    for l in layers:
        if "mix" in stages:
            if l % 2 == 1:
                m.attn(l)
            else:
                m.ssd(l)
        if "ffn" in stages:
            m.ffn(l)
    m.P.barrier()
    m.epilogue()
    m.P.emit()
    stack.close()
    return nc, m


def pack_cf(inp):
    cf = np.zeros((128, 2048), np.float32)
    cf[:, Model.CF_NMIX:Model.CF_NMIX + 32] = inp["norm_mix_g"].reshape(4, 8, 128).transpose(2, 0, 1).reshape(128, 32)
    cf[:, Model.CF_NFFN:Model.CF_NFFN + 32] = inp["norm_ffn_g"].reshape(4, 8, 128).transpose(2, 0, 1).reshape(128, 32)
    cf[:, Model.CF_FCW:Model.CF_FCW + 528] = inp["ffn_conv_w"].reshape(4, 3, 44, 128).transpose(3, 0, 1, 2).reshape(128, 528)
    cf[:, Model.CF_FCB:Model.CF_FCB + 176] = inp["ffn_conv_b"].reshape(4, 44, 128).transpose(2, 0, 1).reshape(128, 176)
    cf[:, 2047] = EPS
    cf[:, 2046] = 1.0
    cf[:, Model.CF_SCW:Model.CF_SCW + 192] = inp["ssd_conv_w"].reshape(2, 4, 24, 128).transpose(3, 0, 1, 2).reshape(128, 192)
    cf[:, Model.CF_SCB:Model.CF_SCB + 48] = inp["ssd_conv_b"].reshape(2, 24, 128).transpose(2, 0, 1).reshape(128, 48)
    cf[:, Model.CF_SNG:Model.CF_SNG + 32] = inp["ssd_norm_g"].reshape(2, 16, 128).transpose(2, 0, 1).reshape(128, 32)
    return cf


def make_in_map(inp, b, m):
    im = {}
    names = set(m.dram.keys())
    if "xT" in names:
        im["xT"] = np.ascontiguousarray(inp["x"][b].T)
    if "cf" in names:
        im["cf"] = pack_cf(inp)
    if "identf" in names:
        im["identf"] = np.eye(128, dtype=np.float32)
    if "srow" in names:
        sr = np.concatenate([inp["ssd_dt_bias"], inp["ssd_a_log"], inp["ssd_d"]], axis=1).astype(np.float32)
        im["srow"] = np.ascontiguousarray(np.broadcast_to(sr[:, None, :], (2, 128, 96)))
    if "masks" in names:
        jj = np.arange(128)
        U = (jj[:, None] <= jj[None, :]).astype(np.float32)
        SL = (jj[:, None] > jj[None, :]).astype(np.float32)
        im["masks"] = np.stack([U, SL, np.ones((128, 128), np.float32)])
    for k in ("ffn_up_w", "ffn_down_w", "attn_in_w", "attn_out_w", "ssd_in_w", "ssd_out_w"):
        if k in names:
            im[k] = inp[k]
    if "pos" in names:
        im["pos"] = np.ascontiguousarray(inp["positions"][b].reshape(16, 128).T).astype(np.int32)
    if "arow" in names:
        ar = np.zeros((2, 520), np.float32)
        for j in range(2):
            ar[j, 0:64] = inp["attn_q_norm_g"][j]
            ar[j, 64:128] = inp["attn_k_norm_g"][j]
            ar[j, 128:192] = inp["attn_lq1"][j]
            ar[j, 192:256] = inp["attn_lk1"][j]
            ar[j, 256:320] = inp["attn_lq2"][j]
            ar[j, 320:384] = inp["attn_lk2"][j]
            ar[j, 384:512] = inp["attn_subln_g"][j]
            ar[j, 512:520] = INV_FREQ
        im["arow"] = np.ascontiguousarray(np.broadcast_to(ar[:, None, :], (2, 128, 520)))
    return im


_CACHE = {}


def kernel(**inputs):
    inp = {k: np.asarray(v) for k, v in inputs.items()}
    if "prog" not in _CACHE:
        _CACHE["prog"] = build([0, 1, 2, 3], ("mix", "ffn"))
    nc, m = _CACHE["prog"]
    shared = make_in_map(inp, 0, m)
    in_maps = []
    for b in range(8):
        im = dict(shared)
        im["xT"] = np.ascontiguousarray(inp["x"][b].T)
        im["pos"] = np.ascontiguousarray(inp["positions"][b].reshape(16, 128).T).astype(np.int32)
        in_maps.append(im)
    res = run_bass_kernel_spmd(nc, in_maps, core_ids=list(range(8)))
    out = np.stack([np.asarray(r["outT"]).T for r in res.results], axis=0)
    return np.ascontiguousarray(out.astype(np.float32))
```

```python
import math
from contextlib import ExitStack

import numpy as np
import concourse.bass as bass
import concourse.mybir as mybir
from concourse.bass_utils import run_bass_kernel_spmd

F32 = mybir.dt.float32
BF16 = mybir.dt.bfloat16
I32 = mybir.dt.int32
AF = mybir.ActivationFunctionType
ALU = mybir.AluOpType

D = 1024
S = 2048
KC = D // 128
DEPTH = 4
EPS = 1e-6
D_FF = 2816
NFC = D_FF // 128
SSD_DI = 2048
SSD_CONVCH = 3072
SSD_IN = 5152
SSD_H = 32
ATT_H = 8

SAME_ENGINE_SYNC = True
INV_FREQ = (500000.0 ** (-np.arange(0, 16, 2, dtype=np.float32) / np.float32(16))).astype(np.float32)


class _Op:
    __slots__ = ("eng", "fn", "reads", "writes", "chan", "deps", "needs_inc", "inc_idx", "waits", "idx")

    def __init__(self, eng, fn, reads, writes, chan):
        self.eng = eng
        self.fn = fn
        self.reads = reads
        self.writes = writes
        self.chan = chan
        self.deps = set()
        self.needs_inc = False
        self.inc_idx = 0
        self.waits = []


class Prog:
    ENGS = ("pe", "act", "dve", "pool", "sp")

    def __init__(self, nc, stack):
        self.nc = nc
        self.stack = stack
        self.ops = []
        self.final_chans = []

    def sb(self, name, shape, dtype):
        return self.stack.enter_context(self.nc.sbuf_tensor(name, list(shape), dtype))

    def ps(self, name, shape, dtype=F32):
        return self.stack.enter_context(self.nc.psum_tensor(name, list(shape), dtype))

    def op(self, eng, fn, reads=(), writes=(), chan=None):
        o = _Op(eng, fn, tuple(reads), tuple(writes), chan)
        o.idx = len(self.ops)
        self.ops.append(o)
        return o

    def barrier(self):
        self.ops.append("barrier")

    def emit(self):
        nc = self.nc
        ops = self.ops
        last_w = {}
        readers = {}
        chan_last = {}
        eng_last = {}
        pending_bar = None
        bar_done = set()
        real = []
        for o in ops:
            if isinstance(o, str):
                pending_bar = list(eng_last.values()) + list(chan_last.values())
                bar_done = set()
                continue
            real.append(o)
            deps = o.deps
            if pending_bar is not None and o.eng not in bar_done:
                deps.update(pending_bar)
                bar_done.add(o.eng)
            for r in o.reads:
                w = last_w.get(r)
                if w is not None:
                    deps.add(w)
            for w_ in o.writes:
                w = last_w.get(w_)
                if w is not None:
                    deps.add(w)
                for rd in readers.get(w_, {}).values():
                    deps.add(rd)
            if o.chan is not None:
                p = chan_last.get(o.chan)
                if p is not None:
                    deps.add(p)
                chan_last[o.chan] = o
            else:
                eng_last[o.eng] = o
            rk = o.eng if o.chan is None else ("c", o.idx)
            for r in o.reads:
                readers.setdefault(r, {})[rk] = o
            for w_ in o.writes:
                last_w[w_] = o
                readers[w_] = {}
            deps.discard(o)
        for o in real:
            for d in o.deps:
                if d.chan is None:
                    if d.eng != o.eng or (SAME_ENGINE_SYNC and d.eng != "pe"):
                        d.needs_inc = True
        eng_cnt = {e: 0 for e in self.ENGS}
        chan_cnt = {}
        last_of_eng = {}
        for o in real:
            if o.chan is not None:
                chan_cnt[o.chan] = chan_cnt.get(o.chan, 0) + 1
                o.inc_idx = 16 * chan_cnt[o.chan]
            else:
                last_of_eng[o.eng] = o
        for e, o in last_of_eng.items():
            o.needs_inc = True
        for o in real:
            if o.chan is None and o.needs_inc:
                eng_cnt[o.eng] += 1
                o.inc_idx = eng_cnt[o.eng]
        sems = {}
        for e in self.ENGS:
            sems[("eng", e)] = self.stack.enter_context(nc.semaphore("sem_" + e))
        for c in chan_cnt:
            sems[("chan", c)] = self.stack.enter_context(nc.semaphore("semc_" + str(c)))
        self.n_sems = len(sems)
        waited = {e: {} for e in self.ENGS}
        for o in real:
            wl = {}
            for d in o.deps:
                if d.chan is not None:
                    key = ("chan", d.chan)
                else:
                    if d.eng == o.eng and (d.eng == "pe" or not SAME_ENGINE_SYNC):
                        continue
                    key = ("eng", d.eng)
                v = d.inc_idx
                if v > wl.get(key, 0):
                    wl[key] = v
            for key, v in wl.items():
                if waited[o.eng].get(key, 0) >= v:
                    continue
                waited[o.eng][key] = v
                o.waits.append((sems[key], v))
        final_waits = []
        for e in self.ENGS:
            if eng_cnt[e] > 0:
                final_waits.append((sems[("eng", e)], eng_cnt[e]))
        for c, n in chan_cnt.items():
            final_waits.append((sems[("chan", c)], 16 * n))
        per_eng = {e: [o for o in real if o.eng == e] for e in self.ENGS}
        self.stats = {e: len(per_eng[e]) for e in self.ENGS}
        self.stats["waits"] = sum(len(o.waits) for o in real)
        self.stats["incs"] = dict(eng_cnt)

        block = self.stack.enter_context(nc.Block())

        def run(eng_obj, lst, fin=None):
            for o in lst:
                for (sm, v) in o.waits:
                    eng_obj.wait_ge(sm, v)
                inst = o.fn(eng_obj)
                if o.chan is not None:
                    inst.then_inc(sems[("chan", o.chan)], 16)
                elif o.needs_inc:
                    inst.then_inc(sems[("eng", o.eng)], 1)
            if fin:
                for (sm, v) in fin:
                    eng_obj.wait_ge(sm, v)

        @block.tensor
        def _(e):
            run(e, per_eng["pe"])

        @block.scalar
        def _(e):
            run(e, per_eng["act"])

        @block.vector
        def _(e):
            run(e, per_eng["dve"])

        @block.gpsimd
        def _(e):
            run(e, per_eng["pool"])

        @block.sync
        def _(e):
            run(e, per_eng["sp"], final_waits)


def pipeline(stages, n, skew=1, order=None):
    ns = len(stages)
    order = order or list(range(ns))
    for t in range(n + (ns - 1) * skew):
        for s in order:
            i = t - s * skew
            if 0 <= i < n:
                stages[s](i)


class Ring:
    def __init__(self, items):
        self.items = items
        self.i = -1

    def next(self):
        self.i = (self.i + 1) % len(self.items)
        return self.items[self.i]


class Model:
    def __init__(self, nc, stack, layers, dbg=None):
        self.nc = nc
        self.stack = stack
        self.P = Prog(nc, stack)
        self.layers = layers
        self.dbg = dbg or {}
        self.dram = {}
        self.uid = 0
        P = self.P
        self.xT = P.sb("xT_sb", [128, KC, S], F32)
        self.consts_f = P.sb("cf_sb", [128, 2048], F32)
        self.ident_f = P.sb("identf_sb", [128, 128], F32)
        self.ident_b = P.sb("identb", [128, 128], BF16)
        self.ones_b = P.sb("onesb", [128, 128], BF16)
        self.AW = 33 * 1024
        self.arena = P.sb("arena", [128, self.AW], F32)
        self.apos = 0
        self.psum = P.ps("psum", [128, 8, 512], F32)
        self.cf_pos = 0

    def din(self, name, shape, dtype=F32):
        t = self.nc.dram_tensor(name, list(shape), dtype, kind="ExternalInput")
        self.dram[name] = t
        return t.ap()

    def dout(self, name, shape, dtype=F32):
        t = self.nc.dram_tensor(name, list(shape), dtype, kind="ExternalOutput")
        self.dram[name] = t
        return t.ap()

    def reset_arena(self):
        self.apos = 0

    def alloc(self, shape, dtype):
        n = 1
        for s_ in shape:
            n *= s_
        nbytes = n * (4 if dtype in (F32, I32) else 2)
        words = (nbytes + 3) // 4
        words = (words + 7) // 8 * 8
        a = self.apos
        self.apos += words
        assert self.apos <= self.AW, f"arena overflow {self.apos} > {self.AW}"
        v = self.arena[:, a:a + words]
        if dtype != F32:
            v = v.bitcast(dtype)
        v = v[:, 0:n]
        if len(shape) == 2:
            v = v.rearrange("p (a b) -> p a b", a=shape[0])
        elif len(shape) == 3:
            v = v.rearrange("p (a b c) -> p a b c", a=shape[0], b=shape[1])
        return v

    def key(self, base):
        self.uid += 1
        return f"{base}#{self.uid}"

    def cf_alloc(self, n):
        a = self.cf_pos
        self.cf_pos += n
        assert self.cf_pos <= 2048
        return self.consts_f[:, a:a + n]

    def dma(self, eng, out, in_, reads, writes, chan, **kw):
        self.P.op(eng, lambda e, out=out, in_=in_, kw=kw: e.dma_start(out=out, in_=in_, **kw),
                  reads=reads, writes=writes, chan=chan)

    def mm(self, out, lhsT, rhs, start, stop, reads, writes):
        self.P.op("pe", lambda e, out=out, lhsT=lhsT, rhs=rhs, start=start, stop=stop:
                  e.matmul(out, lhsT, rhs, start=start, stop=stop), reads=reads, writes=writes)

    def transpose(self, out, in_, ident, reads, writes):
        self.P.op("pe", lambda e, out=out, in_=in_, ident=ident: e.transpose(out, in_, ident),
                  reads=reads, writes=writes)

    def act(self, out, in_, func, reads, writes, scale=1.0, bias=0.0, eng="act"):
        self.P.op(eng, lambda e, out=out, in_=in_, func=func, scale=scale, bias=bias:
                  e.activation(out=out, in_=in_, func=func, scale=scale, bias=bias),
                  reads=reads, writes=writes)

    def ts(self, eng, out, in0, s1, op0, reads, writes, s2=None, op1=None):
        if op1 is None:
            self.P.op(eng, lambda e, out=out, in0=in0, s1=s1, op0=op0:
                      e.tensor_scalar(out=out, in0=in0, scalar1=s1, scalar2=None, op0=op0),
                      reads=reads, writes=writes)
        else:
            self.P.op(eng, lambda e, out=out, in0=in0, s1=s1, op0=op0, s2=s2, op1=op1:
                      e.tensor_scalar(out=out, in0=in0, scalar1=s1, scalar2=s2, op0=op0, op1=op1),
                      reads=reads, writes=writes)

    def stt(self, eng, out, in0, scalar, in1, op0, op1, reads, writes):
        self.P.op(eng, lambda e, out=out, in0=in0, scalar=scalar, in1=in1, op0=op0, op1=op1:
                  e.scalar_tensor_tensor(out=out, in0=in0, scalar=scalar, in1=in1, op0=op0, op1=op1),
                  reads=reads, writes=writes)

    def tt(self, eng, out, in0, in1, op, reads, writes):
        self.P.op(eng, lambda e, out=out, in0=in0, in1=in1, op=op:
                  e.tensor_tensor(out=out, in0=in0, in1=in1, op=op), reads=reads, writes=writes)

    def copy(self, eng, out, in_, reads, writes):
        if eng == "act":
            self.P.op(eng, lambda e, out=out, in_=in_: e.copy(out=out, in_=in_), reads=reads, writes=writes)
        else:
            self.P.op(eng, lambda e, out=out, in_=in_: e.tensor_copy(out=out, in_=in_), reads=reads, writes=writes)

    def memset(self, eng, ap, val, writes):
        self.P.op(eng, lambda e, ap=ap, val=val: e.memset(ap, val), reads=(), writes=writes)

    def recip(self, out, in_, reads, writes):
        self.P.op("dve", lambda e, out=out, in_=in_: e.reciprocal(out=out, in_=in_), reads=reads, writes=writes)

    def declare(self):
        self.d_xT = self.din("xT", [D, S])
        self.d_out = self.dout("outT", [D, S])
        self.d_cf = self.din("cf", [128, 2048])
        self.d_identf = self.din("identf", [128, 128])
        self.d_ffn_up = self.din("ffn_up_w", [DEPTH, D, 2 * D_FF])
        self.d_ffn_dn = self.din("ffn_down_w", [DEPTH, D_FF, D])
        self.d_pos = self.din("pos", [128, 16], I32)
        self.d_arow = self.din("arow", [2, 128, 520])
        self.d_attn_in = self.din("attn_in_w", [2, D, 3 * D])
        self.d_attn_out = self.din("attn_out_w", [2, D, D])
        self.d_srow = self.din("srow", [2, 128, 96])
        self.d_masks = self.din("masks", [3, 128, 128])
        self.d_ssd_in = self.din("ssd_in_w", [2, D, SSD_IN])
        self.d_ssd_out = self.din("ssd_out_w", [2, SSD_DI, D])

    def prologue(self):
        for kc in range(KC):
            self.dma("sp", self.xT[:, kc, :], self.d_xT[kc * 128:(kc + 1) * 128, :],
                     reads=(), writes=[("xT", kc, tb) for tb in range(4)], chan=f"x{kc % 4}")
        self.dma("sp", self.consts_f[:], self.d_cf[:, :], reads=(), writes=["cf"], chan="c0")
        self.dma("sp", self.ident_f[:], self.d_identf[:, :], reads=(), writes=["identf"], chan="c1")
        self.copy("dve", self.ident_b[:], self.ident_f[:], reads=["identf"], writes=["identb"])
        self.U_f = self.P.sb("U_f", [128, 128], F32)
        self.SL_f = self.P.sb("SL_f", [128, 128], F32)
        self.ones_f = self.P.sb("ones_f", [128, 128], F32)
        for i, t_ in enumerate((self.U_f, self.SL_f, self.ones_f)):
            self.dma("sp", t_[:], self.d_masks[i], reads=(), writes=["masks"], chan=f"x{i}")
        self.memset("dve", self.ones_b[:], 1.0, writes=["onesb"])

    def epilogue(self):
        for kc in range(KC):
            self.dma("sp", self.d_out[kc * 128:(kc + 1) * 128, :], self.xT[:, kc, :],
                     reads=[("xT", kc, tb) for tb in range(4)], writes=(), chan=f"x{kc % 4}")

    CF_NMIX = 0
    CF_NFFN = 32
    CF_FCW = 64
    CF_FCB = 592
    CF_END = 768

    def norm(self, gbase, hn, sq, rstd, hn_key="hn"):
        for tb in range(4):
            tsl = slice(tb * 512, (tb + 1) * 512)
            self.act(sq[:, :, :], self.xT[:, :, tsl], AF.Square,
                     reads=[("xT", kc, tb) for kc in range(KC)], writes=["sq"])
            pn = self.psum[:, 7, :]
            for kc in range(KC):
                self.mm(pn, self.ones_b[:], sq[:, kc, :], kc == 0, kc == KC - 1,
                        reads=["sq", "onesb"], writes=["ps7"])
            self.act(rstd[:, :], pn, AF.Sqrt, reads=["ps7", "cf"], writes=["rstd"], scale=1.0 / D,
                     bias=self.eps_col)
            self.recip(rstd[:, :], rstd[:, :], reads=["rstd"], writes=["rstd"])
            for kc in range(KC):
                self.stt("dve", hn[:, kc, tsl], self.xT[:, kc, tsl], self.consts_f[:, gbase + kc:gbase + kc + 1],
                         rstd[:, :], ALU.mult, ALU.mult,
                         reads=[("xT", kc, tb), "rstd", "cf"], writes=[(hn_key, kc, tb)])

    def ffn(self, l):
        self.P.barrier()
        self.reset_arena()
        hn = self.alloc([KC, S], BF16)
        sq = self.alloc([KC, 512], BF16)
        rstd = self.alloc([512], F32)
        J = 4
        wg = [self.alloc([KC, J * 128], BF16) for _ in range(2)]
        wu = [self.alloc([KC, J * 128], BF16) for _ in range(2)]
        wd = [self.alloc([J, D], BF16) for _ in range(1)]
        hb = self.alloc([J, S], BF16)
        gr = [self.alloc([1026], F32) for _ in range(2)]
        ur = [self.alloc([1026], F32) for _ in range(2)]
        tg = self.alloc([1024], F32)
        tu = self.alloc([1024], F32)
        self.norm(self.CF_NFFN + l * 8, hn, sq, rstd)
        groups = [(f0, min(J, NFC - f0)) for f0 in range(0, NFC, J)]
        up = self.d_ffn_up[l].rearrange("(kc p) n -> p kc n", p=128)
        dn = self.d_ffn_dn[l]
        st = {"psi": 0, "pdi": 0}
        units = []
        for gi, (f0, nj) in enumerate(groups):
            for j in range(nj):
                for half in range(2):
                    units.append((gi, j, half))
        first_of = {}
        last_of = {}
        for ui, (gi, j, half) in enumerate(units):
            first_of.setdefault(gi, ui)
            last_of[gi] = ui

        def load_up(gi):
            f0, nj = groups[gi]
            b = gi % 2
            self.dma("pool", wg[b][:, :, 0:nj * 128], up[:, :, f0 * 128:(f0 + nj) * 128],
                     reads=(), writes=[("wg", b)], chan=f"wg{b}")
            self.dma("pool", wu[b][:, :, 0:nj * 128], up[:, :, D_FF + f0 * 128:D_FF + (f0 + nj) * 128],
                     reads=(), writes=[("wu", b)], chan=f"wu{b}")

        def load_dn(gi):
            f0, nj = groups[gi]
            self.dma("pool", wd[0][:, 0:nj, :], dn[f0 * 128:(f0 + nj) * 128, :].rearrange("(j p) n -> p j n", p=128),
                     reads=(), writes=[("wd", 0)], chan="wd0")

        def part_a(ui):
            gi, j, half = units[ui]
            f0, nj = groups[gi]
            b = gi % 2
            ub = ui % 2
            for (raw, wt, wkey, rkey) in ((gr, wg, "wg", "gr"), (ur, wu, "wu", "ur")):
                for t2 in range(2):
                    tb = half * 2 + t2
                    pb = st["psi"] % 4
                    st["psi"] += 1
                    pt = self.psum[:, pb, :]
                    for kc in range(KC):
                        self.mm(pt, wt[b][:, kc, j * 128:(j + 1) * 128], hn[:, kc, tb * 512:(tb + 1) * 512],
                                kc == 0, kc == KC - 1,
                                reads=[(wkey, b), ("hn", kc, tb)], writes=[("psu", pb)])
                    self.copy("act", raw[ub][:, 2 + t2 * 512:2 + (t2 + 1) * 512], pt,
                              reads=[("psu", pb)], writes=[(rkey, ub, 1 + t2)])
                if half == 0:
                    self.memset("dve", raw[ub][:, 0:2], 0.0, writes=[(rkey, ub, 0)])
                else:
                    self.copy("dve", raw[ub][:, 0:2], raw[1 - ub][:, 1024:1026],
                              reads=[(rkey, 1 - ub, 2)], writes=[(rkey, ub, 0)])

        def part_b(ui):
            gi, j, half = units[ui]
            f0, nj = groups[gi]
            fc = f0 + j
            ub = ui % 2
            for (raw, tt_, cc, rkey, tkey) in ((gr, tg, fc, "gr", "tg"), (ur, tu, NFC + fc, "ur", "tu")):
                wcol = lambda k, cc=cc: self.consts_f[:, self.CF_FCW + l * 132 + k * 44 + cc:
                                                      self.CF_FCW + l * 132 + k * 44 + cc + 1]
                bcol = self.consts_f[:, self.CF_FCB + l * 44 + cc:self.CF_FCB + l * 44 + cc + 1]
                rr = [(rkey, ub, 0), (rkey, ub, 1), (rkey, ub, 2), "cf"]
                self.ts("dve", tt_[:, :], raw[ub][:, 2:1026], wcol(2), ALU.mult, reads=rr, writes=[tkey],
                        s2=bcol, op1=ALU.add)
                self.stt("dve", tt_[:, :], raw[ub][:, 1:1025], wcol(1), tt_[:, :], ALU.mult, ALU.add,
                         reads=rr + [tkey], writes=[tkey])
                self.stt("dve", tt_[:, :], raw[ub][:, 0:1024], wcol(0), tt_[:, :], ALU.mult, ALU.add,
                         reads=rr + [tkey], writes=[tkey])
            self.act(tg[:, :], tg[:, :], AF.Silu, reads=["tg"], writes=["tg"])
            self.tt("dve", hb[:, j, half * 1024:(half + 1) * 1024], tg[:, :], tu[:, :], ALU.mult,
                    reads=["tg", "tu"], writes=[("hb", j, half)])

        def down_fn(gi):
            f0, nj = groups[gi]
            for dc in range(KC):
                for tb in range(4):
                    pb = 4 + st["pdi"] % 2
                    st["pdi"] += 1
                    pt = self.psum[:, pb, :]
                    for j in range(nj):
                        self.mm(pt, wd[0][:, j, dc * 128:(dc + 1) * 128], hb[:, j, tb * 512:(tb + 1) * 512],
                                j == 0, j == nj - 1, reads=[("wd", 0), ("hb", j, tb // 2)], writes=[("psd", pb)])
                    xs = self.xT[:, dc, tb * 512:(tb + 1) * 512]
                    self.tt("dve", xs, pt, xs, ALU.add, reads=[("psd", pb), ("xT", dc, tb)], writes=[("xT", dc, tb)])

        load_up(0)
        a_done = set()
        for gi in range(len(groups)):
            load_dn(gi)
            if gi + 1 < len(groups):
                load_up(gi + 1)
            for ui in range(first_of[gi], last_of[gi] + 1):
                if ui not in a_done:
                    part_a(ui)
                    a_done.add(ui)
                part_b(ui)
            if gi + 1 < len(groups):
                nu = first_of[gi + 1]
                part_a(nu)
                a_done.add(nu)
            down_fn(gi)

    def rope_tables(self):
        self.cos_t = self.P.sb("cos_t", [128, 16, 8], F32)
        self.sin_t = self.P.sb("sin_t", [128, 16, 8], F32)
        self.reset_arena()
        posi = self.alloc([16], I32)
        posf = self.alloc([16], F32)
        invf = self.alloc([8], F32)
        ang = self.alloc([16, 8], F32)
        kf = self.alloc([16, 8], F32)
        ki = self.alloc([16, 8], I32)
        y = self.alloc([16, 8], F32)
        t = self.alloc([16, 8], F32)
        self.dma("sp", posi[:, :], self.d_pos[:, :], reads=(), writes=["posi"], chan="c0")
        self.dma("sp", invf[:, :], self.d_arow[0, :, 512:520], reads=(), writes=["invf"], chan="c1")
        self.copy("dve", posf[:, :], posi[:, :], reads=["posi"], writes=["posf"])
        for i in range(16):
            self.ts("dve", ang[:, i, :], invf[:, :], posf[:, i:i + 1], ALU.mult, reads=["posf", "invf"], writes=["ang"])
        TWO_PI = 2.0 * math.pi
        C1 = 6.28125
        C2 = TWO_PI - C1
        self.ts("dve", kf[:, :, :], ang[:, :, :], 1.0 / TWO_PI, ALU.mult, reads=["ang"], writes=["kf"])
        self.copy("dve", ki[:, :, :], kf[:, :, :], reads=["kf"], writes=["ki"])
        self.copy("dve", kf[:, :, :], ki[:, :, :], reads=["ki"], writes=["kf"])
        self.stt("dve", y[:, :, :], kf[:, :, :], -C1, ang[:, :, :], ALU.mult, ALU.add, reads=["kf", "ang"], writes=["y"])
        self.stt("dve", y[:, :, :], kf[:, :, :], -C2, y[:, :, :], ALU.mult, ALU.add, reads=["kf", "y"], writes=["y"])
        for (dst, shift) in ((self.sin_t, 0.0), (self.cos_t, math.pi / 2)):
            z = t
            self.ts("dve", z[:, :, :], y[:, :, :], shift, ALU.add, reads=["y"], writes=["z"])
            self.ts("dve", kf[:, :, :], z[:, :, :], math.pi, ALU.is_gt, reads=["z"], writes=["kf"], s2=-TWO_PI, op1=ALU.mult)
            self.tt("dve", z[:, :, :], z[:, :, :], kf[:, :, :], ALU.add, reads=["z", "kf"], writes=["z"])
            self.ts("dve", kf[:, :, :], z[:, :, :], -math.pi, ALU.is_lt, reads=["z"], writes=["kf"], s2=TWO_PI, op1=ALU.mult)
            self.tt("dve", z[:, :, :], z[:, :, :], kf[:, :, :], ALU.add, reads=["z", "kf"], writes=["z"])
            self.ts("dve", z[:, :, :], z[:, :, :], math.pi, ALU.min, reads=["z"], writes=["z"], s2=-math.pi, op1=ALU.max)
            self.act(dst[:, :, :], z[:, :, :], AF.Sin, reads=["z"], writes=["rope_tab"])

    def attn(self, l):
        j = l // 2
        lambda_init = 0.8 - 0.6 * math.exp(-0.3 * l)
        self.P.barrier()
        self.reset_arena()
        A = self.alloc
        kT = A([8, S], BF16)
        v1 = A([16, 8, 130], BF16)
        hn = A([KC, 512], BF16)
        qT = A([8, 512], BF16)
        oT = A([8, 512], BF16)
        wr = [A([KC, 512], BF16) for _ in range(2)]
        sq = oT
        rstd = A([512], F32)
        arow = A([520], F32)
        xf = [A([512], F32) for _ in range(3)]
        sqt = rstd
        knb = [A([512], BF16) for _ in range(2)]
        ss8s = [A([8], F32) for _ in range(2)]
        ss8 = ss8s[0]
        rt = [A([8, 16], F32) for _ in range(2)]
        Eb = [A([2, 256], BF16) for _ in range(3)]
        oall = [A([8, 128], F32) for _ in range(2)]
        onb = A([8, 128], BF16)
        rc4s = [A([4], F32) for _ in range(2)]
        lam = A([4], F32)
        gq = A([64], F32)
        gsub = A([128], F32)
        lt = A([64], F32)
        self.dma("sp", arow[:, :], self.d_arow[j], reads=(), writes=["arow"], chan="c0")
        self.memset("dve", v1[:, :, :, 128:129], 1.0, writes=["v1ones"])
        self.ts("dve", gq[:, :], arow[:, 0:64], 0.125, ALU.mult, reads=["arow"], writes=["gq"])
        self.ts("dve", gsub[:, :], arow[:, 384:512], 1.0 - lambda_init, ALU.mult, reads=["arow"], writes=["gsub"])
        for i, (a0, b0) in enumerate(((128, 192), (256, 320))):
            self.P.op("dve", lambda e, i=i, a0=a0, b0=b0: e.tensor_tensor(out=lt[:, :], in0=arow[:, a0:a0 + 64], in1=arow[:, b0:b0 + 64], op=ALU.mult),
                      reads=["arow"], writes=["lt"])
            self.P.op("dve", lambda e, i=i: e.reduce_sum(out=lam[:, i:i + 1], in_=lt[:, :], axis=mybir.AxisListType.X),
                      reads=["lt"], writes=["lam"])
        self.act(lam[:, 0:2], lam[:, 0:2], AF.Exp, reads=["lam"], writes=["lam"])
        self.tt("dve", lam[:, 2:3], lam[:, 1:2], lam[:, 0:1], ALU.subtract, reads=["lam"], writes=["lam"])
        self.ts("dve", lam[:, 3:4], lam[:, 2:3], -lambda_init, ALU.add, reads=["lam"], writes=["lam"])
        neg_lam = lam[:, 3:4]

        win = self.d_attn_in[j].rearrange("(kc p) n -> p kc n", p=128)
        wout = self.d_attn_out[j].rearrange("(kc p) n -> p kc n", p=128)
        psb = self.psum.bitcast(BF16)
        wi = 0
        pr = 0

        def nextbank():
            nonlocal pr
            b = pr % 4
            pr += 1
            return b

        for tb in range(4):
            t0 = tb * 512
            self.act(sq[:, :, :], self.xT[:, :, t0:t0 + 512], AF.Square,
                     reads=[("xT", kc, tb) for kc in range(KC)], writes=["oT"])
            b = nextbank()
            pn = self.psum[:, b, :]
            for kc in range(KC):
                self.mm(pn, self.ones_b[:], sq[:, kc, :], kc == 0, kc == KC - 1, reads=["oT", "onesb"], writes=[("ps", b)])
            self.act(rstd[:, :], pn, AF.Sqrt, reads=[("ps", b), "cf"], writes=["rstd"], scale=1.0 / D, bias=self.eps_col)
            self.recip(rstd[:, :], rstd[:, :], reads=["rstd"], writes=["rstd"])
            for kc in range(KC):
                g0 = self.CF_NMIX + l * 8 + kc
                self.stt("dve", hn[:, kc, :], self.xT[:, kc, t0:t0 + 512], self.consts_f[:, g0:g0 + 1], rstd[:, :],
                         ALU.mult, ALU.mult, reads=[("xT", kc, tb), "rstd", "cf"], writes=["hn"])
            units = [(cb, tt_) for cb in (4, 5, 0, 1, 2, 3) for tt_ in range(4)]
            ustate = {}

            def s1(i, units=units, tb=tb, ustate=ustate):
                nonlocal wi
                cb, tt_ = units[i]
                if tt_ == 0:
                    ustate[cb] = wi % 2
                    self.dma("pool", wr[wi % 2][:, :, :], win[:, :, cb * 512:(cb + 1) * 512], reads=(), writes=[("wr", wi % 2)],
                             chan=f"wr{wi % 2}")
                    wi += 1
                w = wr[ustate[cb]]
                wkey = ("wr", ustate[cb])
                kind = cb // 2
                hb = (cb % 2) * 4
                kt = tb * 4 + tt_
                b = nextbank()
                ustate[("b", i)] = b
                pt = self.psum[:, b, :]
                for kc in range(KC):
                    self.mm(pt, hn[:, kc, tt_ * 128:(tt_ + 1) * 128], w[:, kc, :], kc == 0, kc == KC - 1,
                            reads=["hn", wkey], writes=[("ps", b)])
                if kind == 2:
                    self.copy("act", v1[:, kt, hb:hb + 4, 0:128], pt.rearrange("p (h e) -> p h e", h=4),
                              reads=[("ps", b)], writes=[("v1", kt)])
                else:
                    self.copy("act", xf[i % 3][:, :], pt, reads=[("ps", b)], writes=[("xf", i % 3)])

            def s2a(i, units=units, tb=tb):
                cb, tt_ = units[i]
                if cb // 2 == 2:
                    return
                x_ = xf[i % 3]
                xk = ("xf", i % 3)
                s8 = ss8s[i % 2]
                sk = ("ss8", i % 2)
                self.tt("dve", sqt[:, :], x_[:, :], x_[:, :], ALU.mult, reads=[xk], writes=["rstd"])
                self.P.op("dve", lambda e, s8=s8: e.reduce_sum(out=s8[:, :], in_=sqt[:, :].rearrange("p (g d) -> p g d", g=8),
                                                               axis=mybir.AxisListType.X), reads=["rstd"], writes=[sk])
                self.act(s8[:, :], s8[:, :], AF.Sqrt, reads=[sk, "cf"], writes=[sk], scale=1.0 / 64, bias=self.eps_col)

            def s2b(i, units=units, tb=tb):
                cb, tt_ = units[i]
                kind = cb // 2
                if kind == 2:
                    return
                kt = tb * 4 + tt_
                x_ = xf[i % 3]
                xk = ("xf", i % 3)
                s8 = ss8s[i % 2]
                sk = ("ss8", i % 2)
                self.recip(s8[:, :], s8[:, :], reads=[sk], writes=[sk])
                x3 = x_[:, :].rearrange("p (g d) -> p g d", g=8)
                self.tt("dve", x3, x3, s8[:, :].unsqueeze(2).to_broadcast([128, 8, 64]), ALU.mult,
                        reads=[xk, sk], writes=[xk])
                gsrc = gq[:, :] if kind == 0 else arow[:, 64:128]
                self.tt("dve", x3, x3, gsrc.unsqueeze(1).to_broadcast([128, 8, 64]), ALU.mult,
                        reads=[xk, "gq", "arow"], writes=[xk])
                Aap = x3[:, :, 0:8]
                Bap = x3[:, :, 8:16]
                Cc = self.cos_t[:, kt, :].unsqueeze(1).to_broadcast([128, 8, 8])
                Sn = self.sin_t[:, kt, :].unsqueeze(1).to_broadcast([128, 8, 8])
                kk = ("knb", i % 2)
                k3 = knb[i % 2][:, :].rearrange("p (g d) -> p g d", g=8)
                self.copy("dve", knb[i % 2][:, :], x_[:, :], reads=[xk], writes=[kk])
                AB = x3[:, :, 0:16].rearrange("p g (h d) -> p g h d", h=2)
                C4 = self.cos_t[:, kt, :].unsqueeze(1).unsqueeze(1).to_broadcast([128, 8, 2, 8])
                S4 = self.sin_t[:, kt, :].unsqueeze(1).unsqueeze(1).to_broadcast([128, 8, 2, 8])
                rC = rt[0][:, :, :].rearrange("p g (h d) -> p g h d", h=2)
                rS = rt[1][:, :, :].rearrange("p g (h d) -> p g h d", h=2)
                self.tt("pool", rC, AB, C4, ALU.mult, reads=[xk, "rope_tab"], writes=["rt0"])
                self.tt("pool", rS, AB, S4, ALU.mult, reads=[xk, "rope_tab"], writes=["rt1"])
                self.tt("pool", k3[:, :, 0:8], rt[0][:, :, 0:8], rt[1][:, :, 8:16], ALU.subtract, reads=["rt0", "rt1", kk], writes=[kk])
                self.tt("pool", k3[:, :, 8:16], rt[0][:, :, 8:16], rt[1][:, :, 0:8], ALU.add, reads=["rt0", "rt1", kk], writes=[kk])

            def s3(i, units=units, tb=tb):
                cb, tt_ = units[i]
                kind = cb // 2
                if kind == 2:
                    return
                hb = (cb % 2) * 4
                kt = tb * 4 + tt_
                kb_ = knb[i % 2]
                kk = ("knb", i % 2)
                b2 = nextbank()
                for hh in range(4):
                    self.transpose(psb[:, b2, hh * 128:(hh + 1) * 128], kb_[:, hh * 128:(hh + 1) * 128], self.ident_b[:],
                                   reads=[kk, "identb"], writes=[("ps", b2)])
                src_ = psb[:, b2, 0:512].rearrange("p (h t) -> p h t", h=4)
                if kind == 0:
                    self.copy("act", qT[:, hb:hb + 4, tt_ * 128:(tt_ + 1) * 128], src_, reads=[("ps", b2)], writes=[("qT", tt_)])
                else:
                    self.copy("act", kT[:, hb:hb + 4, kt * 128:(kt + 1) * 128], src_, reads=[("ps", b2)], writes=[("kT", kt)])

            pipeline([s1, s2a, s2b, s3], len(units))
            for sub in range(2):
                qt0 = tb * 4 + sub * 2
                nk = qt0 + 2
                items = [(h, kt) for h in range(ATT_H) for kt in range(nk)]

                def stage_a(i, items=items, qt0=qt0, sub=sub):
                    h, kt = items[i]
                    if kt <= qt0:
                        qlo, n = 0, 256
                    else:
                        qlo, n = 128, 128
                    sb_ = i % 2
                    E = Eb[i % 3]
                    ek = ("E", i % 3)
                    for c in range(2):
                        self.mm(self.psum[:, 2 * sb_ + c, 0:n], kT[c * 64:(c + 1) * 64, h, kt * 128:(kt + 1) * 128],
                                qT[c * 64:(c + 1) * 64, h, sub * 256 + qlo:sub * 256 + qlo + n], True, True,
                                reads=[("kT", kt), ("qT", sub * 2), ("qT", sub * 2 + 1)], writes=[("ps", 2 * sb_ + c)])
                    self.act(E[:, :, qlo:qlo + n], self.psum[:, 2 * sb_:2 * sb_ + 2, 0:n], AF.Exp,
                             reads=[("ps", 2 * sb_), ("ps", 2 * sb_ + 1)], writes=[ek])
                    if kt >= qt0:
                        c0 = (kt - qt0) * 128
                        self.memset("dve", E[64:128, :, c0:c0 + 64], 0.0, writes=[ek])

                def stage_b(i, items=items, qt0=qt0, sub=sub, nk=nk):
                    h, kt = items[i]
                    E = Eb[i % 3]
                    ek = ("E", i % 3)
                    for qi in range(2):
                        qt = qt0 + qi
                        if kt > qt:
                            continue
                        for c in range(2):
                            ob = 4 + qi * 2 + c
                            self.mm(self.psum[:, ob, 0:129], E[:, c, qi * 128:(qi + 1) * 128], v1[:, kt, h, 0:129],
                                    kt == 0, kt == qt, reads=[ek, ("v1", kt), "v1ones"], writes=[("ps", ob)])
                    if kt != nk - 1:
                        return
                    rc4 = rc4s[h % 2]
                    rk4 = ("rc4", h % 2)
                    self.recip(rc4[:, :], self.psum[:, 4:8, 128], reads=[("ps", 4), ("ps", 5), ("ps", 6), ("ps", 7)], writes=[rk4])
                    r2 = rc4[:, :].rearrange("p (q c) -> p q c", c=2)
                    self.ts("dve", r2[:, :, 1], r2[:, :, 1], neg_lam, ALU.mult, reads=[rk4, "lam"], writes=[rk4])
                    for qi in range(2):
                        ob = 4 + qi * 2
                        oa = oall[qi]
                        self.act(oa[:, h, :], self.psum[:, ob, 0:128], AF.Copy, reads=[("ps", ob), rk4],
                                 writes=[("oall", qi, h)], scale=rc4[:, 2 * qi:2 * qi + 1])
                        self.stt("dve", oa[:, h, :], self.psum[:, ob + 1, 0:128], rc4[:, 2 * qi + 1:2 * qi + 2], oa[:, h, :],
                                 ALU.mult, ALU.add,
                                 reads=[("ps", ob + 1), rk4, ("oall", qi, h)], writes=[("oall", qi, h)])

                pipeline([stage_a, stage_b], len(items), skew=2)
                for qi in range(2):
                    oa = oall[qi]
                    ok = ("oall", qi)
                    okh = [("oall", qi, hh_) for hh_ in range(8)]
                    o2 = oa[:, :, :]
                    sq4 = sqt[:, :].rearrange("p (g d) -> p g d", g=4)
                    for hf in range(2):
                        self.tt("dve", sq4, oa[:, hf * 4:hf * 4 + 4, :], oa[:, hf * 4:hf * 4 + 4, :], ALU.mult,
                                reads=okh + [("ss8", 0)], writes=["rstd"])
                        self.P.op("dve", lambda e, hf=hf, sq4=sq4: e.reduce_sum(out=ss8[:, hf * 4:hf * 4 + 4], in_=sq4,
                                                                                axis=mybir.AxisListType.X),
                                  reads=["rstd"], writes=[("ss8", 0)])
                    self.act(ss8[:, :], ss8[:, :], AF.Sqrt, reads=[("ss8", 0), "cf"], writes=[("ss8", 0)], scale=1.0 / 128, bias=self.eps_col)
                    self.recip(ss8[:, :], ss8[:, :], reads=[("ss8", 0)], writes=[("ss8", 0)])
                    self.tt("dve", o2, o2, ss8[:, :].unsqueeze(2).to_broadcast([128, 8, 128]), ALU.mult, reads=okh + [("ss8", 0)], writes=okh)
                    self.tt("dve", onb[:, :, :], o2, gsub[:, :].unsqueeze(1).to_broadcast([128, 8, 128]), ALU.mult,
                            reads=okh + ["gsub"], writes=["onb"])
                    b2 = nextbank()
                    for hh in range(8):
                        self.transpose(psb[:, b2, hh * 128:(hh + 1) * 128], onb[:, hh, :], self.ident_b[:],
                                       reads=["onb", "identb"], writes=[("ps", b2)])
                    c0 = sub * 256 + qi * 128
                    self.copy("act", oT[:, :, c0:c0 + 128], psb[:, b2, :].rearrange("p (h t) -> p h t", h=8),
                              reads=[("ps", b2)], writes=["oT"])
            for half in range(2):
                w = wr[wi % 2]
                wkey = ("wr", wi % 2)
                wi += 1
                self.dma("pool", w[:, :, :], wout[:, :, half * 512:(half + 1) * 512], reads=(), writes=[wkey], chan=f"wr{(wi - 1) % 2}")
                for dcc in range(4):
                    dc = half * 4 + dcc
                    b = nextbank()
                    pt = self.psum[:, b, :]
                    for kc in range(KC):
                        self.mm(pt, w[:, kc, dcc * 128:(dcc + 1) * 128], oT[:, kc, :], kc == 0, kc == KC - 1,
                                reads=[wkey, "oT"], writes=[("ps", b)])
                    xs = self.xT[:, dc, t0:t0 + 512]
                    self.tt("dve", xs, pt, xs, ALU.add, reads=[("ps", b), ("xT", dc, tb)], writes=[("xT", dc, tb)])

    CF_SCW = 768
    CF_SCB = 960
    CF_SNG = 1008

    def ssd(self, l):
        j = l // 2
        P = self.P
        P.barrier()
        self.reset_arena()
        A = self.alloc
        sz = A([4, 2048], BF16)
        xdt = A([4, 2048], BF16)
        BT = A([4, 512], BF16)
        CT = A([4, 512], BF16)
        Btok = A([4, 4, 128], BF16)
        dt = A([4, 32], F32)
        dA = A([4, 32], F32)
        halo = A([24, 4], F32)
        hT = A([2048], F32)
        hTb = A([2048], BF16)
        srow = A([96], F32)
        arow_ = A([32], F32)
        wdt = A([KC, 32], BF16)
        wr = [A([KC, 512], BF16) for _ in range(2)]
        ygT = A([16, 512], BF16)
        mark = self.apos
        self.dma("sp", srow[:, :], self.d_srow[j], reads=(), writes=["srow"], chan="c0")
        self.act(arow_[:, :], srow[:, 32:64], AF.Exp, reads=["srow"], writes=["arow_"])
        self.ts("dve", arow_[:, :], arow_[:, :], -1.0, ALU.mult, reads=["arow_"], writes=["arow_"])
        self.memset("dve", halo[:, :, :], 0.0, writes=[("halo", ch_) for ch_ in range(24)])
        self.memset("dve", hT[:, :], 0.0, writes=[("hT", g_) for g_ in range(4)])
        self.memset("dve", hTb[:, :], 0.0, writes=[("hTb", g_) for g_ in range(4)])
        win = self.d_ssd_in[j].rearrange("(kc p) n -> p kc n", p=128)
        wout = self.d_ssd_out[j].rearrange("(kc p) n -> p kc n", p=128)
        self.dma("pool", wdt[:, :, :], win[:, :, 5120:5152], reads=(), writes=["wdt"], chan="wdt")
        psb = self.psum.bitcast(BF16)
        pr = 0
        wblocks = []
        for tb_ in range(4):
            for cb in range(4):
                wblocks.append(("in", win[:, :, cb * 512:(cb + 1) * 512]))
            for cb in range(6):
                wblocks.append(("in", win[:, :, 2048 + cb * 512:2048 + (cb + 1) * 512]))
            for q4 in range(4):
                wblocks.append(("out", wout[:, :, q4 * 256:(q4 + 1) * 256]))
        wstate = {"next": 0, "use": 0}

        def wfetch():
            k = wstate["next"]
            if k >= len(wblocks):
                return
            kind, src_ = wblocks[k]
            dst = wr[k % 2][:, :, :] if kind == "in" else wr[k % 2].rearrange("p a (b c) -> p (a b) c", b=2)
            self.dma("pool", dst, src_, reads=(), writes=[("wr", k % 2)], chan=f"wr{k % 2}")
            wstate["next"] = k + 1

        def wuse():
            k = wstate["use"]
            wstate["use"] = k + 1
            while wstate["next"] <= min(k + 1, len(wblocks) - 1):
                wfetch()
            return wr[k % 2], ("wr", k % 2)

        def nextbank(nb=4, base=0):
            nonlocal pr
            b = base + pr % nb
            pr += 1
            return b

        for tb in range(4):
            t0 = tb * 512
            P.barrier()
            self.apos = mark
            hn = A([KC, 512], BF16)
            sq = A([KC, 512], BF16)
            rstd = A([512], F32)
            raw = [A([516], F32) for _ in range(2)]
            tcv = [A([512], F32) for _ in range(3)]
            xch = [A([512], BF16) for _ in range(2)]
            dtt = A([4, 32], F32)
            self.act(sq[:, :, :], self.xT[:, :, t0:t0 + 512], AF.Square,
                     reads=[("xT", kc, tb) for kc in range(KC)], writes=["sq"])
            b = nextbank()
            pn = self.psum[:, b, :]
            for kc in range(KC):
                self.mm(pn, self.ones_b[:], sq[:, kc, :], kc == 0, kc == KC - 1, reads=["sq", "onesb"], writes=[("ps", b)])
            self.act(rstd[:, :], pn, AF.Sqrt, reads=[("ps", b), "cf"], writes=["rstd"], scale=1.0 / D, bias=self.eps_col)
            self.recip(rstd[:, :], rstd[:, :], reads=["rstd"], writes=["rstd"])
            for kc in range(KC):
                g0 = self.CF_NMIX + l * 8 + kc
                self.stt("dve", hn[:, kc, :], self.xT[:, kc, t0:t0 + 512], self.consts_f[:, g0:g0 + 1], rstd[:, :],
                         ALU.mult, ALU.mult, reads=[("xT", kc, tb), "rstd", "cf"], writes=["hn"])
            for tt_ in range(4):
                b = nextbank()
                pt = self.psum[:, b, 0:32]
                for kc in range(KC):
                    self.mm(pt, hn[:, kc, tt_ * 128:(tt_ + 1) * 128], wdt[:, kc, :], kc == 0, kc == KC - 1,
                            reads=["hn", "wdt"], writes=[("ps", b)])
                self.tt("dve", dtt[:, tt_, :], pt, srow[:, 0:32], ALU.add, reads=[("ps", b), "srow"], writes=["dtt"])
            d3 = dt[:, :, :]
            x3 = dtt[:, :, :]
            self.stt("dve", d3, x3, -1.0, x3, ALU.mult, ALU.max, reads=["dtt"], writes=["dt"])
            self.act(d3, d3, AF.Exp, reads=["dt"], writes=["dt"], scale=-1.0)
            self.act(d3, d3, AF.Ln, reads=["dt"], writes=["dt"], bias=self.one_col)
            self.ts("dve", x3, x3, 0.0, ALU.max, reads=["dtt"], writes=["dtt"])
            self.tt("dve", d3, d3, x3, ALU.add, reads=["dt", "dtt"], writes=["dt"])
            self.tt("dve", dA[:, :, :], d3, arow_[:, :].unsqueeze(1).to_broadcast([128, 4, 32]), ALU.mult,
                    reads=["dt", "arow_"], writes=["dA"])
            for cb in range(4):
                w, wkey = wuse()
                for tt_ in range(4):
                    b = nextbank()
                    pt = self.psum[:, b, :]
                    for kc in range(KC):
                        self.mm(pt, hn[:, kc, tt_ * 128:(tt_ + 1) * 128], w[:, kc, :], kc == 0, kc == KC - 1,
                                reads=["hn", wkey], writes=[("ps", b)])
                    self.act(sz[:, tt_, cb * 512:(cb + 1) * 512], pt, AF.Silu, reads=[("ps", b)], writes=[("sz", tt_)])
            wslot = {}

            def p1(ch):
                cb, cl = divmod(ch, 4)
                if cl == 0:
                    wslot[cb] = wuse()
                w, wkey = wslot[cb]
                r_ = raw[ch % 2]
                rk = ("raw", ch % 2)
                b = nextbank()
                pt = self.psum[:, b, :]
                for kc in range(KC):
                    self.mm(pt, w[:, kc, cl * 128:(cl + 1) * 128], hn[:, kc, :], kc == 0, kc == KC - 1,
                            reads=["hn", wkey], writes=[("ps", b)])
                self.copy("pool", r_[:, 0:3], halo[:, ch, 0:3], reads=[("halo", ch)], writes=[rk])
                self.copy("act", r_[:, 3:515], pt, reads=[("ps", b)], writes=[rk])
                wc3 = self.consts_f[:, self.CF_SCW + j * 96 + 3 * 24 + ch:self.CF_SCW + j * 96 + 3 * 24 + ch + 1]
                bc3 = self.consts_f[:, self.CF_SCB + j * 24 + ch:self.CF_SCB + j * 24 + ch + 1]
                self.act(tcv[ch % 3][:, :], pt, AF.Identity, reads=[("ps", b), "cf"], writes=[("tcv", ch % 3)], scale=wc3, bias=bc3)
                self.copy("pool", halo[:, ch, 0:3], r_[:, 512:515], reads=[rk], writes=[("halo", ch)])

            def p2a(ch):
                r_ = raw[ch % 2]
                rk = ("raw", ch % 2)
                tc_ = tcv[ch % 3]
                tk = ("tcv", ch % 3)
                wc = lambda k, ch=ch: self.consts_f[:, self.CF_SCW + j * 96 + k * 24 + ch:self.CF_SCW + j * 96 + k * 24 + ch + 1]
                for k in (2, 1, 0):
                    self.stt("dve", tc_[:, :], r_[:, k:k + 512], wc(k), tc_[:, :], ALU.mult, ALU.add,
                             reads=[rk, "cf", tk], writes=[tk])

            def p2b(ch):
                tc_ = tcv[ch % 3]
                tk = ("tcv", ch % 3)
                if ch < 16:
                    self.act(xch[ch % 2][:, :], tc_[:, :], AF.Silu, reads=[tk], writes=[("xch", ch % 2)])
                elif ch < 20:
                    self.act(BT[:, ch - 16, :], tc_[:, :], AF.Silu, reads=[tk], writes=[("BT", ch - 16)])
                else:
                    self.act(CT[:, ch - 20, :], tc_[:, :], AF.Silu, reads=[tk], writes=[("CT", ch - 20)])

            def p3(ch):
                if ch < 16:
                    xc_ = xch[ch % 2]
                    xk = ("xch", ch % 2)
                    b2 = nextbank()
                    for tt_ in range(4):
                        self.transpose(psb[:, b2, tt_ * 128:(tt_ + 1) * 128], xc_[:, tt_ * 128:(tt_ + 1) * 128], self.ident_b[:],
                                       reads=[xk, "identb"], writes=[("ps", b2)])
                    self.tt("dve", xdt[:, :, ch * 128:(ch + 1) * 128].rearrange("p t (h d) -> p t h d", h=2),
                            psb[:, b2, 0:512].rearrange("p (t h d) -> p t h d", t=4, h=2),
                            dt[:, :, 2 * ch:2 * ch + 2].unsqueeze(3).to_broadcast([128, 4, 2, 64]), ALU.mult,
                            reads=[("ps", b2), "dt"], writes=[("xdt", ch)])
                elif ch < 20:
                    g = ch - 16
                    b2 = nextbank()
                    for tt_ in range(4):
                        self.transpose(psb[:, b2, tt_ * 128:(tt_ + 1) * 128], BT[:, g, tt_ * 128:(tt_ + 1) * 128], self.ident_b[:],
                                       reads=[("BT", g), "identb"], writes=[("ps", b2)])
                    self.copy("act", Btok[:, :, g, :], psb[:, b2, 0:512].rearrange("p (t n) -> p t n", t=4),
                              reads=[("ps", b2)], writes=[("Btok", g)])

            pipeline([p1, p2a, p2b, p3], 24)
            P.barrier()
            self.apos = mark
            Rg = [A([8, 128], F32) for _ in range(2)]
            Lm = [A([8, 128], BF16) for _ in range(2)]
            MT = [A([8, 128], BF16) for _ in range(2)]
            CBU = A([4, 128], BF16)
            xdd = A([2048], BF16)
            ytmp = [A([512], F32) for _ in range(2)]
            ynb = [A([512], BF16) for _ in range(2)]
            sm = A([64], F32)
            ea = A([32], F32)
            dec = A([32], F32)
            etot = A([32], F32)
            ddt = A([32], F32)
            ss1 = A([4], F32)
            tmpD = [A([8, 128], BF16) for _ in range(3)]
            ddt4 = A([4, 32], F32)
            pre = [dict(sm=A([64], F32), ea=A([32], F32), dec=A([32], F32), etot=A([32], F32), ddt=A([32], F32),
                        CBU=A([4, 128], BF16)) for _ in range(2)]

            def preamble(c):
                pp = pre[c % 2]
                pk = lambda n: (n, c % 2)
                self.mm(self.psum[:, 0, 0:32], self.U_f[:], dA[:, c, :], True, True, reads=["dA", "masks"], writes=[("ps", 0)])
                self.mm(self.psum[:, 0, 32:64], self.ones_f[:], dA[:, c, :], True, True, reads=["dA", "masks"], writes=[("ps", 0)])
                self.copy("dve", pp["sm"][:, :], self.psum[:, 0, 0:64], reads=[("ps", 0)], writes=[pk("sm")])
                self.act(pp["ea"][:, :], pp["sm"][:, 0:32], AF.Exp, reads=[pk("sm")], writes=[pk("ea")])
                self.tt("dve", pp["dec"][:, :], pp["sm"][:, 32:64], pp["sm"][:, 0:32], ALU.subtract, reads=[pk("sm")], writes=[pk("dec")])
                self.act(pp["dec"][:, :], pp["dec"][:, :], AF.Exp, reads=[pk("dec")], writes=[pk("dec")])
                self.act(pp["etot"][:, :], pp["sm"][:, 32:64], AF.Exp, reads=[pk("sm")], writes=[pk("etot")])
                for g in range(4):
                    self.mm(self.psum[:, 1, g * 128:(g + 1) * 128], BT[:, g, c * 128:(c + 1) * 128], CT[:, g, c * 128:(c + 1) * 128],
                            True, True, reads=[("BT", g), ("CT", g)], writes=[("ps", 1)])
                self.tt("dve", pp["CBU"][:, :, :], self.psum[:, 1, :].rearrange("p (g l) -> p g l", g=4),
                        self.U_f[:].unsqueeze(1).to_broadcast([128, 4, 128]), ALU.mult, reads=[("ps", 1), "masks"], writes=[pk("CBU")])

            def state_update(c):
                pp = pre[c % 2]
                pk = lambda n: (n, c % 2)
                for g in range(4):
                    sbk = 7 if g % 2 == 0 else 0
                    self.mm(self.psum[:, sbk, :], Btok[:, c, g, :], xdd[:, g * 512:(g + 1) * 512], True, True,
                            reads=[("Btok", g), "xdd"], writes=[("ps", sbk)])
                    hg = hT[:, g * 512:(g + 1) * 512]
                    self.tt("dve", hg.rearrange("p (r d) -> p r d", r=8), hg.rearrange("p (r d) -> p r d", r=8),
                            pp["etot"][:, g * 8:(g + 1) * 8].unsqueeze(2).to_broadcast([128, 8, 64]), ALU.mult,
                            reads=[("hT", g), pk("etot")], writes=[("hT", g)])
                    self.tt("dve", hg, self.psum[:, sbk, :], hg, ALU.add, reads=[("ps", sbk), ("hT", g)], writes=[("hT", g)])
                    self.copy("act", hTb[:, g * 512:(g + 1) * 512], hg, reads=[("hT", g)], writes=[("hTb", g)])

            self.recip(ddt4[:, :, :], dt[:, :, :], reads=["dt"], writes=["ddt4"])
            self.tt("dve", ddt4[:, :, :], ddt4[:, :, :], srow[:, 64:96].unsqueeze(1).to_broadcast([128, 4, 32]), ALU.mult,
                    reads=["ddt4", "srow"], writes=["ddt4"])

            def s0(i):
                c, g = divmod(i, 4)
                kt = tb * 4 + c
                if g == 0:
                    preamble(c)
                bi = i % 2
                R = Rg[bi]
                self.tt("pool", R[:, :, :], dA[:, c, g * 8:(g + 1) * 8].unsqueeze(2).to_broadcast([128, 8, 128]),
                        self.U_f[:].unsqueeze(1).to_broadcast([128, 8, 128]), ALU.mult, reads=["dA", "masks"], writes=[("R", bi)])
                TD = tmpD[i % 3]
                self.tt("pool", TD[:, :, :], self.ident_b[:].unsqueeze(1).to_broadcast([128, 8, 128]),
                        ddt4[:, c, g * 8:(g + 1) * 8].unsqueeze(2).to_broadcast([128, 8, 128]), ALU.mult,
                        reads=["identb", "ddt4"], writes=[("tmpD", i % 3)])
                if g == 3 and kt != 15:
                    pp = pre[c % 2]
                    self.tt("pool", xdd[:, :].rearrange("p (h d) -> p h d", h=32), xdt[:, c, :].rearrange("p (h d) -> p h d", h=32),
                            pp["dec"][:, :].unsqueeze(2).to_broadcast([128, 32, 64]), ALU.mult,
                            reads=[("xdt", q) for q in range(16)] + [("dec", c % 2)], writes=["xdd"])

            def s1(i):
                bi = i % 2
                R = Rg[bi]
                for k2 in range(2):
                    self.mm(self.psum[:, 2 + k2, :], self.SL_f[:], R[:, 4 * k2:4 * k2 + 4, :].rearrange("p r l -> p (r l)"),
                            True, True, reads=[("R", bi), "masks"], writes=[("ps", 2 + k2)])
                self.act(Lm[bi][:, :, :], self.psum[:, 2:4, :].rearrange("p b (r l) -> p (b r) l", r=4), AF.Exp,
                         reads=[("ps", 2), ("ps", 3)], writes=[("Lm", bi)])

            def s2(i):
                c, g = divmod(i, 4)
                bi = i % 2
                pp = pre[c % 2]
                M_ = MT[bi]
                self.tt("dve", M_[:, :, :], Lm[bi][:, :, :], pp["CBU"][:, g, :].unsqueeze(1).to_broadcast([128, 8, 128]), ALU.mult,
                        reads=[("Lm", bi), ("CBU", c % 2)], writes=[("MT", bi)])
                self.tt("pool", M_[:, :, :], M_[:, :, :], tmpD[i % 3][:, :, :], ALU.add, reads=[("MT", bi), ("tmpD", i % 3)],
                        writes=[("MT", bi)])

            def s3(i):
                c, g = divmod(i, 4)
                kt = tb * 4 + c
                bi = i % 2
                M_ = MT[bi]
                for r in range(8):
                    h = g * 8 + r
                    self.mm(self.psum[:, 4, r * 64:(r + 1) * 64], M_[:, r, :], xdt[:, c, h * 64:(h + 1) * 64], True, True,
                            reads=[("MT", bi), ("xdt", h // 2)], writes=[("ps", 4)])
                self.mm(self.psum[:, 5, :], CT[:, g, c * 128:(c + 1) * 128], hTb[:, g * 512:(g + 1) * 512], True, True,
                        reads=[("CT", g), ("hTb", g)], writes=[("ps", 5)])
                if g == 3 and kt != 15:
                    state_update(c)

            def s4(i):
                c, g = divmod(i, 4)
                bi = i % 2
                pp = pre[c % 2]
                yt = ytmp[bi]
                yk = ("ytmp", bi)
                self.tt("dve", yt[:, :].rearrange("p (r d) -> p r d", r=8), self.psum[:, 5, :].rearrange("p (r d) -> p r d", r=8),
                        pp["ea"][:, g * 8:(g + 1) * 8].unsqueeze(2).to_broadcast([128, 8, 64]), ALU.mult,
                        reads=[("ps", 5), ("ea", c % 2)], writes=[yk])
                self.tt("dve", yt[:, :], self.psum[:, 4, :], yt[:, :], ALU.add, reads=[("ps", 4), yk], writes=[yk])
                self.tt("dve", yt[:, :], yt[:, :], sz[:, c, g * 512:(g + 1) * 512], ALU.mult, reads=[yk, ("sz", c)], writes=[yk])
                sk = ("ss1", i % 4)
                s_ = ss1[:, i % 4:i % 4 + 1]
                self.P.op("act", lambda e, yt=yt, s_=s_, bi=bi: e.activation(out=ynb[bi][:, :], in_=yt[:, :], func=AF.Square, accum_out=s_),
                          reads=[yk], writes=[("ynb", bi), sk])
                self.act(s_, s_, AF.Sqrt, reads=[sk, "cf"], writes=[sk], scale=1.0 / 512, bias=self.eps_col)
                self.recip(s_, s_, reads=[sk], writes=[sk])
                self.act(ynb[bi][:, :], yt[:, :], AF.Copy, reads=[yk, sk], writes=[("ynb", bi)], scale=s_)

            def s5(i):
                c, g = divmod(i, 4)
                bi = i % 2
                yb = ynb[bi]
                for q in range(4):
                    self.transpose(psb[:, 6, q * 128:(q + 1) * 128], yb[:, q * 128:(q + 1) * 128], self.ident_b[:],
                                   reads=[("ynb", bi), "identb"], writes=[("ps", 6)])
                for q in range(4):
                    cc = g * 4 + q
                    g0 = self.CF_SNG + j * 16 + cc
                    self.act(ygT[:, cc, c * 128:(c + 1) * 128], psb[:, 6, q * 128:(q + 1) * 128], AF.Copy,
                             reads=[("ps", 6), "cf"], writes=["ygT"], scale=self.consts_f[:, g0:g0 + 1])

            pipeline([s0, s1, s2, s3, s4, s5], 16, order=[5, 4, 3, 2, 1, 0])
            for q4 in range(4):
                w, wkey = wuse()
                w16 = w.rearrange("p a (b c) -> p (a b) c", b=2)
                for dcc in range(2):
                    dc = q4 * 2 + dcc
                    b = nextbank(2, 2)
                    pt = self.psum[:, b, :]
                    for kc in range(16):
                        self.mm(pt, w16[:, kc, dcc * 128:(dcc + 1) * 128], ygT[:, kc, :], kc == 0, kc == 15,
                                reads=[wkey, "ygT"], writes=[("ps", b)])
                    xs = self.xT[:, dc, t0:t0 + 512]
                    self.tt("dve", xs, pt, xs, ALU.add, reads=[("ps", b), ("xT", dc, tb)], writes=[("xT", dc, tb)])

def build(layers, stages=("ffn",)):
    nc = bass.Bass("TRN2", target_bir_lowering=False)
    stack = ExitStack()
    m = Model(nc, stack, layers)
    m.declare()
    m.eps_col = None
    m.prologue()
    m.eps_col = m.consts_f[:, 2047:2048]
    m.one_col = m.consts_f[:, 2046:2047]
    m.rope_tables()
    for l in layers:
        if "mix" in stages:
            if l % 2 == 1:
                m.attn(l)
            else:
                m.ssd(l)
        if "ffn" in stages:
            m.ffn(l)
    m.P.barrier()
    m.epilogue()
    m.P.emit()
    stack.close()
    return nc, m


def pack_cf(inp):
    cf = np.zeros((128, 2048), np.float32)
    cf[:, Model.CF_NMIX:Model.CF_NMIX + 32] = inp["norm_mix_g"].reshape(4, 8, 128).transpose(2, 0, 1).reshape(128, 32)
    cf[:, Model.CF_NFFN:Model.CF_NFFN + 32] = inp["norm_ffn_g"].reshape(4, 8, 128).transpose(2, 0, 1).reshape(128, 32)
    cf[:, Model.CF_FCW:Model.CF_FCW + 528] = inp["ffn_conv_w"].reshape(4, 3, 44, 128).transpose(3, 0, 1, 2).reshape(128, 528)
    cf[:, Model.CF_FCB:Model.CF_FCB + 176] = inp["ffn_conv_b"].reshape(4, 44, 128).transpose(2, 0, 1).reshape(128, 176)
    cf[:, 2047] = EPS
    cf[:, 2046] = 1.0
    cf[:, Model.CF_SCW:Model.CF_SCW + 192] = inp["ssd_conv_w"].reshape(2, 4, 24, 128).transpose(3, 0, 1, 2).reshape(128, 192)
    cf[:, Model.CF_SCB:Model.CF_SCB + 48] = inp["ssd_conv_b"].reshape(2, 24, 128).transpose(2, 0, 1).reshape(128, 48)
    cf[:, Model.CF_SNG:Model.CF_SNG + 32] = inp["ssd_norm_g"].reshape(2, 16, 128).transpose(2, 0, 1).reshape(128, 32)
    return cf


def make_in_map(inp, b, m):
    im = {}
    names = set(m.dram.keys())
    if "xT" in names:
        im["xT"] = np.ascontiguousarray(inp["x"][b].T)
    if "cf" in names:
        im["cf"] = pack_cf(inp)
    if "identf" in names:
        im["identf"] = np.eye(128, dtype=np.float32)
    if "srow" in names:
        sr = np.concatenate([inp["ssd_dt_bias"], inp["ssd_a_log"], inp["ssd_d"]], axis=1).astype(np.float32)
        im["srow"] = np.ascontiguousarray(np.broadcast_to(sr[:, None, :], (2, 128, 96)))
    if "masks" in names:
        jj = np.arange(128)
        U = (jj[:, None] <= jj[None, :]).astype(np.float32)
        SL = (jj[:, None] > jj[None, :]).astype(np.float32)
        im["masks"] = np.stack([U, SL, np.ones((128, 128), np.float32)])
    for k in ("ffn_up_w", "ffn_down_w", "attn_in_w", "attn_out_w", "ssd_in_w", "ssd_out_w"):
        if k in names:
            im[k] = inp[k]
    if "pos" in names:
        im["pos"] = np.ascontiguousarray(inp["positions"][b].reshape(16, 128).T).astype(np.int32)
    if "arow" in names:
        ar = np.zeros((2, 520), np.float32)
        for j in range(2):
            ar[j, 0:64] = inp["attn_q_norm_g"][j]
            ar[j, 64:128] = inp["attn_k_norm_g"][j]
            ar[j, 128:192] = inp["attn_lq1"][j]
            ar[j, 192:256] = inp["attn_lk1"][j]
            ar[j, 256:320] = inp["attn_lq2"][j]
            ar[j, 320:384] = inp["attn_lk2"][j]
            ar[j, 384:512] = inp["attn_subln_g"][j]
            ar[j, 512:520] = INV_FREQ
        im["arow"] = np.ascontiguousarray(np.broadcast_to(ar[:, None, :], (2, 128, 520)))
    return im


_CACHE = {}


def kernel(**inputs):
    inp = {k: np.asarray(v) for k, v in inputs.items()}
    if "prog" not in _CACHE:
        _CACHE["prog"] = build([0, 1, 2, 3], ("mix", "ffn"))
    nc, m = _CACHE["prog"]
    shared = make_in_map(inp, 0, m)
    in_maps = []
    for b in range(8):
        im = dict(shared)
        im["xT"] = np.ascontiguousarray(inp["x"][b].T)
        im["pos"] = np.ascontiguousarray(inp["positions"][b].reshape(16, 128).T).astype(np.int32)
        in_maps.append(im)
    res = run_bass_kernel_spmd(nc, in_maps, core_ids=list(range(8)))
    out = np.stack([np.asarray(r["outT"]).T for r in res.results], axis=0)
    return np.ascontiguousarray(out.astype(np.float32))
```
